# Optimizing a Trainium2 kernel written in Bass

```python
import math
import jax, jax.numpy as jnp
from jax import lax
import numpy as np

D_MODEL = 1024
BATCH = 8
SEQ = 2048
DEPTH = 2
DEC_BATCH = 128
DEC_SEQ = 1
PAST_LEN = 16384
PAGE_SIZE = 128

N_EVEN = (DEPTH + 1) // 2
N_ODD = DEPTH // 2
D_A = D_MODEL // 2
S5_GROUP = 16
N_GA = D_A // S5_GROUP
P_A = 64
D_B = D_MODEL // 2
N_HB = 4
HD_B = D_B // N_HB
CHUNK = 128
D_IN_AB = D_A + 2 * D_B
D_C = D_MODEL
K_C = 31
D_FF = ((8 * D_MODEL) // 3 + 127) // 128 * 128
K_F = 3
ALPHA = (2.0 * DEPTH) ** 0.25
BETA = (8.0 * DEPTH) ** -0.25
LN_EPS = 1e-5

kernel_name = 'hybrid_s5_gmlp_conformer_step'


def _layer_norm(x, g, b):
    xf = x.astype(jnp.float32)
    mu = jnp.mean(xf, axis=-1, keepdims=True)
    xc = xf - mu
    var = jnp.mean(xc * xc, axis=-1, keepdims=True)
    y = xc * lax.rsqrt(var + LN_EPS) * g.astype(jnp.float32) + b.astype(jnp.float32)
    return y.astype(x.dtype)


def _causal_dwconv(x, cache, w, b):
    k = w.shape[0]
    if cache is None:
        cache = jnp.zeros((x.shape[0], k - 1, x.shape[2]), x.dtype)
    xc = jnp.concatenate([cache.astype(x.dtype), x], axis=1)
    y = lax.conv_general_dilated(xc, w[:, None, :].astype(x.dtype), window_strides=(1,), padding='VALID',
                                 dimension_numbers=('NWC', 'WIO', 'NWC'), feature_group_count=x.shape[2])
    return y + b.astype(x.dtype), xc[:, xc.shape[1] - (k - 1):]


def _scan_combine(e1, e2):
    a1, b1 = e1
    a2, b2 = e2
    return a1 * a2, a2 * b1 + b2


def _s5(u, s0, lam_re, lam_im, log_dt, b_re, b_im, c_re, c_im, d_skip, glu_w, glu_b):
    bn, t, _ = u.shape
    f32 = jnp.float32
    uf = u.astype(f32)
    lam = lax.complex(lam_re.astype(f32), lam_im.astype(f32))
    dt = jnp.exp(log_dt.astype(f32))[:, None]
    lam_bar = jnp.exp(lam * dt)
    b_bar = ((lam_bar - 1.0) / lam)[:, :, None] * lax.complex(b_re.astype(f32), b_im.astype(f32))
    c_mat = lax.complex(c_re.astype(f32), c_im.astype(f32))
    ug = uf.reshape(bn, t, N_GA, S5_GROUP).astype(jnp.complex64)
    bu = jnp.einsum('gpc,btgc->btgp', b_bar, ug)
    if s0 is not None:
        bu = bu.at[:, 0].add(lam_bar[None] * s0)
    a = jnp.broadcast_to(lam_bar[None, None], (1, t, N_GA, P_A))
    _, s = lax.associative_scan(_scan_combine, (a, bu), axis=1)
    y = jnp.real(jnp.einsum('gcp,btgp->btgc', c_mat, s)).reshape(bn, t, D_A) + d_skip.astype(f32) * uf
    g = jax.nn.gelu(y)
    out = g * jax.nn.sigmoid(g @ glu_w.astype(f32) + glu_b.astype(f32))
    return out.astype(u.dtype), s[:, t - 1]


def _spatial_gate(v, w_s, b_s):
    bn, t, _ = v.shape
    l = min(t, CHUNK)
    nc = t // l
    mask = jnp.tril(jnp.ones((l, l), v.dtype))
    w = w_s[:, :l, :l].astype(v.dtype) * mask
    vh = v.reshape(bn, nc, l, N_HB, HD_B)
    out = jnp.einsum('hts,bcshd->bcthd', w, vh) + jnp.transpose(b_s[:, :l]).astype(v.dtype)[:, :, None]
    return out.reshape(bn, t, D_B)


def _even_mixer(x, s0, w_in, lam_re, lam_im, log_dt, b_re, b_im, c_re, c_im, d_skip, glu_w, glu_b,
                sgu_g, sgu_bn, sgu_w, sgu_b, w_out):
    z = x @ w_in
    u_a = z[..., :D_A]
    u_b = z[..., D_A:D_A + D_B]
    v_b = z[..., D_A + D_B:]
    y_a, s_last = _s5(u_a, s0, lam_re, lam_im, log_dt, b_re, b_im, c_re, c_im, d_skip, glu_w, glu_b)
    v_n = _layer_norm(v_b, sgu_g, sgu_bn)
    y_b = u_b * _spatial_gate(v_n, sgu_w, sgu_b)
    out = jnp.concatenate([y_a, y_b], axis=-1) @ w_out
    return out, s_last, v_n


def _odd_mixer(x, cache, w_in, conv_w, conv_b, ln_g, ln_b, w_out):
    z = x @ w_in
    g = z[..., :D_C] * jax.nn.sigmoid(z[..., D_C:])
    h, new_cache = _causal_dwconv(g, cache, conv_w, conv_b)
    h = jax.nn.silu(_layer_norm(h, ln_g, ln_b))
    return h @ w_out, new_cache


def _conv_ffn(x, cache, w_gate, w_up, conv_w, conv_b, w_down):
    gate, new_cache = _causal_dwconv(x @ w_gate, cache, conv_w, conv_b)
    return (jax.nn.silu(gate) * (x @ w_up)) @ w_down, new_cache


def setup_inputs(seed: int = 0) -> dict:
    key = jax.random.key(seed)
    ks = jax.random.split(key, 40)
    f32 = jnp.float32

    def nrm(i, shape, scale):
        return scale * jax.random.normal(ks[i], shape, f32)

    n_idx = jnp.arange(P_A, dtype=f32)
    return {
        'x_prompt': nrm(0, (BATCH, SEQ, D_MODEL), 1.0),
        'x_sample': nrm(1, (DEC_BATCH, DEC_SEQ, D_MODEL), 1.0),
        'state_a_re': nrm(2, (N_EVEN, DEC_BATCH, N_GA, P_A), 0.1),
        'state_a_im': nrm(3, (N_EVEN, DEC_BATCH, N_GA, P_A), 0.1),
        'cache_c_conv': nrm(4, (N_ODD, DEC_BATCH, K_C - 1, D_C), 0.5),
        'cache_ffn_conv': nrm(5, (DEPTH, DEC_BATCH, K_F - 1, D_FF), 1.0),
        'w_in_ab': nrm(6, (N_EVEN, D_MODEL, D_IN_AB), D_MODEL ** -0.5),
        's5_lam_re': -0.5 + nrm(7, (N_EVEN, N_GA, P_A), 0.01),
        's5_lam_im': math.pi * n_idx + nrm(8, (N_EVEN, N_GA, P_A), 0.01),
        's5_log_dt': jax.random.uniform(ks[9], (N_EVEN, N_GA), f32, math.log(1e-3), math.log(1e-1)),
        's5_b_re': nrm(10, (N_EVEN, N_GA, P_A, S5_GROUP), (2.0 * S5_GROUP) ** -0.5),
        's5_b_im': nrm(11, (N_EVEN, N_GA, P_A, S5_GROUP), (2.0 * S5_GROUP) ** -0.5),
        's5_c_re': nrm(12, (N_EVEN, N_GA, S5_GROUP, P_A), (2.0 * P_A) ** -0.5),
        's5_c_im': nrm(13, (N_EVEN, N_GA, S5_GROUP, P_A), (2.0 * P_A) ** -0.5),
        's5_d': nrm(14, (N_EVEN, D_A), 1.0),
        's5_glu_w': nrm(15, (N_EVEN, D_A, D_A), D_A ** -0.5),
        's5_glu_b': nrm(16, (N_EVEN, D_A), 0.02),
        'sgu_ln_g': 1.0 + nrm(17, (N_EVEN, D_B), 0.02),
        'sgu_ln_b': nrm(18, (N_EVEN, D_B), 0.02),
        'sgu_w': nrm(19, (N_EVEN, N_HB, CHUNK, CHUNK), CHUNK ** -0.5),
        'sgu_b': 1.0 + nrm(20, (N_EVEN, N_HB, CHUNK), 0.02),
        'w_out_ab': nrm(21, (N_EVEN, D_A + D_B, D_MODEL), BETA * (D_A + D_B) ** -0.5),
        'w_in_c': nrm(22, (N_ODD, D_MODEL, 2 * D_C), D_MODEL ** -0.5),
        'conv_c_w': nrm(23, (N_ODD, K_C, D_C), K_C ** -0.5),
        'conv_c_b': nrm(24, (N_ODD, D_C), 0.02),
        'ln_c_g': 1.0 + nrm(25, (N_ODD, D_C), 0.02),
        'ln_c_b': nrm(26, (N_ODD, D_C), 0.02),
        'w_out_c': nrm(27, (N_ODD, D_C, D_MODEL), BETA * D_C ** -0.5),
        'ffn_w_gate': nrm(28, (DEPTH, D_MODEL, D_FF), D_MODEL ** -0.5),
        'ffn_w_up': nrm(29, (DEPTH, D_MODEL, D_FF), D_MODEL ** -0.5),
        'ffn_conv_w': nrm(30, (DEPTH, K_F, D_FF), K_F ** -0.5),
        'ffn_conv_b': nrm(31, (DEPTH, D_FF), 0.02),
        'ffn_w_down': nrm(32, (DEPTH, D_FF, D_MODEL), BETA * D_FF ** -0.5),
        'ln_mix_g': 1.0 + nrm(33, (DEPTH, D_MODEL), 0.02),
        'ln_mix_b': nrm(34, (DEPTH, D_MODEL), 0.02),
        'ln_ffn_g': 1.0 + nrm(35, (DEPTH, D_MODEL), 0.02),
        'ln_ffn_b': nrm(36, (DEPTH, D_MODEL), 0.02),
    }


def reference(x_prompt, x_sample, state_a_re, state_a_im, cache_c_conv, cache_ffn_conv,
              w_in_ab, s5_lam_re, s5_lam_im, s5_log_dt, s5_b_re, s5_b_im, s5_c_re, s5_c_im, s5_d,
              s5_glu_w, s5_glu_b, sgu_ln_g, sgu_ln_b, sgu_w, sgu_b, w_out_ab,
              w_in_c, conv_c_w, conv_c_b, ln_c_g, ln_c_b, w_out_c,
              ffn_w_gate, ffn_w_up, ffn_conv_w, ffn_conv_b, ffn_w_down,
              ln_mix_g, ln_mix_b, ln_ffn_g, ln_ffn_b):
    f32 = jnp.float32
    xp, xs = x_prompt, x_sample
    sa_re_p, sa_im_p, sa_re_s, sa_im_s, sb_v_s = [], [], [], [], []
    cc_p, cc_s, cf_p, cf_s = [], [], [], []
    for l in range(DEPTH):
        if l % 2 == 0:
            e = l // 2
            ep = (w_in_ab[e], s5_lam_re[e], s5_lam_im[e], s5_log_dt[e], s5_b_re[e], s5_b_im[e],
                  s5_c_re[e], s5_c_im[e], s5_d[e], s5_glu_w[e], s5_glu_b[e],
                  sgu_ln_g[e], sgu_ln_b[e], sgu_w[e], sgu_b[e], w_out_ab[e])
            s0 = lax.complex(state_a_re[e].astype(f32), state_a_im[e].astype(f32))
            mp, sp_last, _ = _even_mixer(xp, None, *ep)
            ms, ss_last, v_new = _even_mixer(xs, s0, *ep)
            sa_re_p.append(jnp.real(sp_last))
            sa_im_p.append(jnp.imag(sp_last))
            sa_re_s.append(jnp.real(ss_last))
            sa_im_s.append(jnp.imag(ss_last))
            sb_v_s.append(v_new)
        else:
            o = l // 2
            op = (w_in_c[o], conv_c_w[o], conv_c_b[o], ln_c_g[o], ln_c_b[o], w_out_c[o])
            mp, cp = _odd_mixer(xp, None, *op)
            ms, cs = _odd_mixer(xs, cache_c_conv[o], *op)
            cc_p.append(cp)
            cc_s.append(cs)
        xp = _layer_norm(ALPHA * xp + mp, ln_mix_g[l], ln_mix_b[l])
        xs = _layer_norm(ALPHA * xs + ms, ln_mix_g[l], ln_mix_b[l])
        fp_ = (ffn_w_gate[l], ffn_w_up[l], ffn_conv_w[l], ffn_conv_b[l], ffn_w_down[l])
        hp, fcp = _conv_ffn(xp, None, *fp_)
        hs, fcs = _conv_ffn(xs, cache_ffn_conv[l], *fp_)
        cf_p.append(fcp)
        cf_s.append(fcs)
        xp = _layer_norm(ALPHA * xp + hp, ln_ffn_g[l], ln_ffn_b[l])
        xs = _layer_norm(ALPHA * xs + hs, ln_ffn_g[l], ln_ffn_b[l])
    return (xp, xs,
            jnp.stack(sa_re_p), jnp.stack(sa_im_p), jnp.stack(sa_re_s), jnp.stack(sa_im_s),
            jnp.stack(sb_v_s),
            jnp.stack(cc_p), jnp.stack(cc_s),
            jnp.stack(cf_p), jnp.stack(cf_s))
```

```python
import math
import numpy as np
from contextlib import ExitStack
import concourse.bass as bass
import concourse.mybir as mybir
from concourse.bass_utils import run_bass_kernel_spmd

F32 = mybir.dt.float32
BF16 = mybir.dt.bfloat16
I32 = mybir.dt.int32
ALU = mybir.AluOpType
AF = mybir.ActivationFunctionType
AX = mybir.AxisListType

ENGS = ("pe", "act", "dve", "pool", "sp")
ALPHA = (2.0 * 2) ** 0.25
LN_EPS = 1e-5
NCORES = 8
T = 2048
NS = 16
DM = 1024
DFF = 2816
NH = 22
LCH = 128
ARENA_WORDS = 53100


class Buf:
    __slots__ = ("name", "w", "r", "excl")

    def __init__(self, name="", excl=False):
        self.name = name
        self.w = None
        self.r = {}
        self.excl = excl


def _tok_newer(a, b):
    if a[0] == "e":
        return a[2] > b[2]
    return a[2] > b[2]


def inherit(newb, olds):
    for ob in olds:
        toks = list(ob.r.items())
        if ob.w is not None:
            t = ob.w
            key = t[1] if t[0] == "e" else ("d", t[1])
            toks.append((key, t))
        for key, t in toks:
            cur = newb.r.get(key)
            if cur is None or _tok_newer(t, cur):
                newb.r[key] = t


class KB:
    def __init__(self, nc, n_dma_sems=12):
        self.nc = nc
        self.stack = ExitStack()
        self.prog = []
        self.n_eng = {e: 0 for e in ENGS}
        self.n_dma_sems = n_dma_sems
        self.dma_slot = {e: 0 for e in ENGS}
        self.dma_last = {}
        self.dma_val = {}

    def sb(self, name, shape, dtype):
        return self.stack.enter_context(self.nc.sbuf_tensor(name, list(shape), dtype))

    def ps(self, name, shape, dtype=F32):
        return self.stack.enter_context(self.nc.psum_tensor(name, list(shape), dtype))

    def _deps(self, eng, reads, writes):
        deps = []
        for b in reads:
            if b.w is not None:
                deps.append(b.w)
            if b.excl:
                for key, tok in b.r.items():
                    if key != eng:
                        deps.append(tok)
        for b in writes:
            if b.w is not None:
                deps.append(b.w)
            deps.extend(b.r.values())
        return deps

    def op(self, eng, fn, reads=(), writes=()):
        deps = self._deps(eng, reads, writes)
        idx = self.n_eng[eng]
        self.n_eng[eng] += 1
        tok = ("e", eng, idx)
        self.prog.append((eng, fn, deps, tok, False))
        for b in reads:
            b.r[eng] = tok
        for b in writes:
            b.w = tok
            b.r = {}
        return tok

    def dma(self, eng, out, in_, reads=(), writes=(), **kw):
        deps = self._deps(eng, reads, writes)
        slot = self.dma_slot[eng]
        self.dma_slot[eng] = (slot + 1) % self.n_dma_sems
        key = (eng, slot)
        if key in self.dma_last:
            deps.append(self.dma_last[key])
        val = self.dma_val.get(key, 0) + 16
        self.dma_val[key] = val
        tok = ("d", key, val)
        self.dma_last[key] = tok
        self.n_eng[eng] += 1

        def fn(e, out=out, in_=in_, kw=kw):
            return e.dma_start(out=out, in_=in_, **kw)
        self.prog.append((eng, fn, deps, tok, True))
        for b in reads:
            b.r[("d", key)] = tok
        for b in writes:
            b.w = tok
            b.r = {}
        return tok

    def emit(self, final_eng="sp"):
        nc = self.nc
        fin_deps = list(self.dma_last.values())
        self.prog.append((final_eng, None, fin_deps, ("e", final_eng, self.n_eng[final_eng]), False))
        marked = set()
        for eng, fn, deps, tok, is_dma in self.prog:
            for d in deps:
                if d[0] == "e":
                    if d[1] == "pe" and eng == "pe":
                        continue
                    marked.add(d)
        cnt = {e: 0 for e in ENGS}
        count_at = {}
        for eng, fn, deps, tok, is_dma in self.prog:
            if tok[0] == "e" and tok in marked:
                cnt[eng] += 1
                count_at[tok] = cnt[eng]
        esem = {e: self.stack.enter_context(nc.semaphore("s_" + e)) for e in ENGS}
        dsem = {}
        for key in self.dma_val:
            dsem[key] = self.stack.enter_context(nc.semaphore("d_%s%d" % key))
        streams = {e: [] for e in ENGS}
        clock = {e: {} for e in ENGS}
        tok_clock = {}
        n_wait = 0
        for eng, fn, deps, tok, is_dma in self.prog:
            ck = clock[eng]
            for d in deps:
                if d[0] == "e":
                    if d[1] == "pe" and eng == "pe":
                        continue
                    key = ("e", d[1])
                    val = count_at[d]
                    sem = esem[d[1]]
                else:
                    key = ("d", d[1])
                    val = d[2]
                    sem = dsem[d[1]]
                if ck.get(key, 0) >= val:
                    continue
                streams[eng].append(("w", sem, val))
                n_wait += 1
                for k2, v2 in tok_clock[d].items():
                    if ck.get(k2, 0) < v2:
                        ck[k2] = v2
            if fn is None:
                continue
            if is_dma:
                streams[eng].append(("d", fn, dsem[tok[1]]))
                c2 = dict(ck)
                c2[("d", tok[1])] = tok[2]
                tok_clock[tok] = c2
            else:
                inc = tok in marked
                streams[eng].append(("o", fn, esem[eng] if inc else None))
                if inc:
                    c2 = dict(ck)
                    c2[("e", eng)] = count_at[tok]
                    tok_clock[tok] = c2
        self.stats = dict(n_wait=n_wait, n_inst={e: len(streams[e]) for e in ENGS}, marked=len(marked), cnt=cnt)

        def run(e, items):
            for it in items:
                if it[0] == "w":
                    e.wait_ge(it[1], it[2])
                elif it[0] == "d":
                    it[1](e).then_inc(it[2], 16)
                else:
                    ins = it[1](e)
                    if it[2] is not None:
                        ins.then_inc(it[2], 1)

        with nc.Block() as block:
            @block.tensor
            def _(e):
                run(e, streams["pe"])

            @block.scalar
            def _(e):
                run(e, streams["act"])

            @block.vector
            def _(e):
                run(e, streams["dve"])

            @block.gpsimd
            def _(e):
                run(e, streams["pool"])

            @block.sync
            def _(e):
                run(e, streams["sp"])
        self.stack.close()


class Arena:
    def __init__(self, t, nwords):
        self.t = t
        self.n = nwords
        self.off = 0
        self.recs = []

    def mark(self):
        return self.off

    def release(self, m):
        self.off = m

    def _alloc(self, words, drop=True):
        words = (words + 7) // 8 * 8
        s = self.off
        assert s + words <= self.n, ("arena overflow", s, words, self.n)
        self.off = s + words
        rec = [s, s + words, [], None, False]
        olds = []
        keep = []
        for r in self.recs:
            if r[0] < rec[1] and rec[0] < r[1]:
                olds.extend(r[2])
                if drop and r[0] >= rec[0] and r[1] <= rec[1] and not r[4]:
                    continue
            keep.append(r)
        self.recs = keep
        self.recs.append(rec)
        rec[3] = olds
        return rec

    def f32(self, shape):
        n = int(np.prod(shape[1:]))
        rec = self._alloc(n)
        ap = self.t[:shape[0], rec[0]:rec[0] + n]
        return self._shape(ap, shape), rec

    def bf16(self, shape):
        n = int(np.prod(shape[1:]))
        rec = self._alloc((n + 1) // 2)
        ap = self.t[:shape[0], rec[0]:rec[0] + (n + 1) // 2].bitcast(BF16)[:, 0:n]
        return self._shape(ap, shape), rec

    def at(self, word_off, words):
        save = self.off
        self.off = word_off
        rec = self._alloc(words, drop=False)
        self.off = save
        return rec

    @staticmethod
    def _shape(ap, shape):
        if len(shape) == 2:
            return ap
        if len(shape) == 3:
            return ap.rearrange("p (a b) -> p a b", a=shape[1])
        if len(shape) == 4:
            return ap.rearrange("p (a b c) -> p a b c", a=shape[1], b=shape[2])
        raise ValueError(shape)


def nbuf(rec, name=""):
    b = Buf(name)
    inherit(b, rec[3])
    rec[2].append(b)
    return b


PC = {}


def _pcol_layout():
    off = 0
    for l in range(2):
        for nm in ("mixg", "mixb", "ffng", "ffnb"):
            PC[(nm, l)] = off
            off += 8
    for nm in ("lncg", "lncb", "convcb"):
        PC[nm] = off
        off += 8
    PC["convcw"] = off
    off += 31 * 8
    for l in range(2):
        PC[("fcw", l)] = off
        off += 3 * NH
        PC[("fcb", l)] = off
        off += NH
    PC["s5d"] = off
    off += 4
    PC["glub"] = off
    off += 4
    PC["n"] = off


_pcol_layout()


def _cols(v, ntile):
    return np.ascontiguousarray(v.reshape(ntile, 128).T)


def host_params(inp):
    pc = np.zeros((128, PC["n"]), np.float32)
    for l in range(2):
        pc[:, PC[("mixg", l)]:PC[("mixg", l)] + 8] = _cols(inp["ln_mix_g"][l], 8)
        pc[:, PC[("mixb", l)]:PC[("mixb", l)] + 8] = _cols(inp["ln_mix_b"][l], 8)
        pc[:, PC[("ffng", l)]:PC[("ffng", l)] + 8] = _cols(inp["ln_ffn_g"][l], 8)
        pc[:, PC[("ffnb", l)]:PC[("ffnb", l)] + 8] = _cols(inp["ln_ffn_b"][l], 8)
        for k in range(3):
            pc[:, PC[("fcw", l)] + k * NH:PC[("fcw", l)] + (k + 1) * NH] = _cols(inp["ffn_conv_w"][l, k], NH)
        pc[:, PC[("fcb", l)]:PC[("fcb", l)] + NH] = _cols(inp["ffn_conv_b"][l], NH)
    pc[:, PC["lncg"]:PC["lncg"] + 8] = _cols(inp["ln_c_g"][0], 8)
    pc[:, PC["lncb"]:PC["lncb"] + 8] = _cols(inp["ln_c_b"][0], 8)
    pc[:, PC["convcb"]:PC["convcb"] + 8] = _cols(inp["conv_c_b"][0], 8)
    for k in range(31):
        pc[:, PC["convcw"] + k * 8:PC["convcw"] + (k + 1) * 8] = _cols(inp["conv_c_w"][0, k], 8)
    pc[:, PC["s5d"]:PC["s5d"] + 4] = _cols(inp["s5_d"][0], 4)
    pc[:, PC["glub"]:PC["glub"] + 4] = _cols(inp["s5_glu_b"][0], 4)

    def pairl(a):
        return np.ascontiguousarray(a.reshape(16, 2, 64).transpose(1, 2, 0).reshape(128, 16))
    out = {"pcol": pc}
    out["lamre"] = pairl(inp["s5_lam_re"][0])
    out["lamim"] = pairl(inp["s5_lam_im"][0])
    out["logdt"] = pairl(np.broadcast_to(inp["s5_log_dt"][0][:, None], (32, 64)))

    def bx(b):
        o = np.zeros((128, 16, 128), np.float32)
        for p in range(16):
            for gi in range(2):
                r0 = 32 * (p % 4) + 16 * gi
                o[64 * gi:64 * gi + 64, p, r0:r0 + 16] = b[2 * p + gi]
        return o

    def cx(c):
        o = np.zeros((128, 16, 128), np.float32)
        for p in range(16):
            for gi in range(2):
                r0 = 32 * (p % 4) + 16 * gi
                o[64 * gi:64 * gi + 64, p, r0:r0 + 16] = c[2 * p + gi].T
        return o
    cwt = np.zeros((32, DM), np.float32)
    cwt[:31] = inp["conv_c_w"][0]
    out["convst"] = np.ascontiguousarray(
        cwt.reshape(8, 4, 8, 4, 32).transpose(1, 4, 2, 3, 0).reshape(128, 256))
    out["brex"] = bx(inp["s5_b_re"][0])
    out["bimx"] = bx(inp["s5_b_im"][0])
    out["crex"] = cx(inp["s5_c_re"][0])
    out["cimx"] = cx(inp["s5_c_im"][0])
    w = inp["sgu_w"][0]
    out["sguwT"] = np.ascontiguousarray(w.transpose(2, 0, 1))
    out["sgubc"] = np.ascontiguousarray(np.broadcast_to(inp["sgu_b"][0][None], (128, 4, 128)))
    out["sgw00"] = np.ascontiguousarray(np.broadcast_to(np.repeat(w[:, 0, 0], 128)[None], (NS, 512)))
    out["sgb0"] = np.ascontiguousarray(np.broadcast_to(np.repeat(inp["sgu_b"][0][:, 0], 128)[None], (NS, 512)))
    out["sglng"] = np.ascontiguousarray(np.broadcast_to(inp["sgu_ln_g"][0][None], (128, 512)))
    out["sglnb"] = np.ascontiguousarray(np.broadcast_to(inp["sgu_ln_b"][0][None], (128, 512)))
    for k in ("w_in_ab", "s5_glu_w", "w_out_ab", "w_in_c", "w_out_c"):
        out[k] = np.ascontiguousarray(inp[k][0])
    for k in ("ffn_w_gate", "ffn_w_up", "ffn_w_down"):
        out[k] = np.ascontiguousarray(inp[k])
    return out


IN_SHAPES = {
    "xp": [T, DM], "xs": [NS, DM], "sare": [NS, 2048], "saim": [NS, 2048], "ccc": [NS * 30, DM],
    "cfc": [2, NS * 2, DFF], "pcol": [128, PC["n"]], "lamre": [128, 16], "lamim": [128, 16], "logdt": [128, 16],
    "brex": [128, 16, 128], "bimx": [128, 16, 128], "crex": [128, 16, 128], "cimx": [128, 16, 128],
    "sguwT": [128, 4, 128], "sgubc": [128, 4, 128], "sgw00": [NS, 512], "sgb0": [NS, 512],
    "sglng": [128, 512], "sglnb": [128, 512], "convst": [128, 256],
    "w_in_ab": [1024, 1536], "s5_glu_w": [512, 512], "w_out_ab": [1024, 1024], "w_in_c": [1024, 2048],
    "w_out_c": [1024, 1024], "ffn_w_gate": [2, 1024, DFF], "ffn_w_up": [2, 1024, DFF], "ffn_w_down": [2, DFF, 1024],
}
OUT_SHAPES = {
    "yp": [T, DM], "ys": [NS, DM], "sarp": [16, 128], "saip": [16, 128], "sars": [NS, 2048], "sais": [NS, 2048],
    "sbv": [NS, 512], "ccp": [30, DM], "ccs": [NS * 30, DM], "cfp": [2, 2, DFF], "cfs": [2, NS * 2, DFF],
}


def build_program(stages=("me", "f0", "mo", "f1")):
    nc = bass.Bass("TRN2", target_bir_lowering=False)
    kb = KB(nc)
    D = {}
    compute = any(x in stages for x in ("me", "f0", "mo", "f1"))
    for k, s in IN_SHAPES.items():
        if not compute and (k.startswith("w_") or k.startswith("ffn_w") or k == "s5_glu_w"):
            continue
        D[k] = nc.dram_tensor(k, list(s), F32, kind="ExternalInput").ap()
    for k, s in OUT_SHAPES.items():
        D[k] = nc.dram_tensor(k, list(s), F32, kind="ExternalOutput").ap()

    GSD = [(nc.dram_tensor("gsd%d" % i_, [128, 1056], BF16).ap(), Buf("gsd%d" % i_)) for i_ in range(3)]
    S5C = nc.dram_tensor("s5c_scratch", [128, 8192], F32).ap()
    b_S5C = Buf("s5c")
    S5CB = nc.dram_tensor("s5cb_scratch", [128, 16384], BF16).ap()
    b_S5CB = Buf("s5cb")
    AR_t = kb.sb("arena", [128, ARENA_WORDS], F32)
    AR = Arena(AR_t, ARENA_WORDS)
    banks = [kb.ps("bank%d" % i, [128, 512], F32) for i in range(8)]
    bank_buf = [Buf("bank%d" % i, excl=True) for i in range(8)]
    bank_rr = [0]

    def bank():
        i = bank_rr[0]
        bank_rr[0] = (i + 1) % 8
        return banks[i][:, :], bank_buf[i]

    def MM(out, lhsT, rhs, start, stop, reads, writes):
        kb.op("pe", lambda e: e.matmul(out, lhsT, rhs, start=start, stop=stop), reads, writes)

    def TR(out, in_, ident_ap, reads, writes):
        kb.op("pe", lambda e: e.transpose(out, in_, ident_ap), reads, writes)

    def TT(eng, out, a, b, op, reads, writes):
        kb.op(eng, lambda e: e.tensor_tensor(out, a, b, op), reads, writes)

    def STT(out, a, s, b, op0, op1, reads, writes):
        kb.op("dve", lambda e: e.scalar_tensor_tensor(out, a, s, b, op0, op1), reads, writes)

    def TS(eng, out, a, s1, s2, op0, op1, reads, writes):
        if s2 is None:
            kb.op(eng, lambda e: e.tensor_scalar(out, a, s1, None, op0), reads, writes)
        else:
            kb.op(eng, lambda e: e.tensor_scalar(out, a, s1, s2, op0, op1), reads, writes)

    def ACT(out, in_, func, reads, writes, bias=None, scale=None):
        kw = {}
        if bias is not None:
            kw["bias"] = bias
        if scale is not None:
            kw["scale"] = scale
        kb.op("act", lambda e: e.activation(out, in_, func, **kw), reads, writes)

    def CP(eng, out, in_, reads, writes):
        if eng == "act":
            kb.op("act", lambda e: e.copy(out, in_), reads, writes)
        else:
            kb.op(eng, lambda e: e.tensor_copy(out, in_), reads, writes)

    def MEMSET(eng, out, val, writes):
        kb.op(eng, lambda e: e.memset(out, val), (), writes)

    NCOL = 1024 + NS
    X32, rX32 = AR.f32([128, 8, NCOL])
    XB, rXB = AR.bf16([128, 8, NCOL])
    XB_WORD0 = rXB[0]
    XB_WORDS = rXB[1] - rXB[0]
    rX32[4] = True
    rXB[4] = True
    slots = []
    for i in range(3):
        ap, rec = AR.bf16([128, 4096])
        slots.append((ap, nbuf(rec, "slot%d" % i)))
    slot_rr = [0]
    ident, r_ = AR.f32([128, 128]); b_ident = nbuf(r_)
    identb, r_ = AR.bf16([128, 128]); b_identb = nbuf(r_)
    ones_c, r_ = AR.bf16([128, 128]); b_ones_c = nbuf(r_)
    pcol, r_ = AR.f32([128, PC["n"]]); b_pcol = nbuf(r_)
    cst, r_ = AR.f32([128, 8]); b_cst = nbuf(r_)
    m32, r_ = AR.bf16([128, 32]); b_m32 = nbuf(r_)
    m32f, r_ = AR.f32([128, 32]); b_m32f = nbuf(r_)
    convst, r_ = AR.f32([128, 256]); b_convst = nbuf(r_)
    stg = []
    for i in range(2):
        ap, rec = AR.f32([128, 1024])
        stg.append((ap, nbuf(rec, "stg%d" % i)))
    stg_rr = [0]
    ln_rb4, r_ = AR.bf16([128, 4, 512]); b_ln_rb4 = nbuf(r_)
    ln_sq4, r_ = AR.bf16([128, 4, 512]); b_ln_sq4 = nbuf(r_)
    ln_t = [AR.f32([128, 512]) for _ in range(2)]
    ln_t2 = [AR.f32([128, 512]) for _ in range(2)]
    ln_st = [AR.f32([128, 512]) for _ in range(2)]
    ln_t = [(a, nbuf(r)) for a, r in ln_t]
    ln_t2 = [(a, nbuf(r)) for a, r in ln_t2]
    ln_st = [(a, nbuf(r)) for a, r in ln_st]
    ln_rr = [0]
    halo_ffn, r_ = AR.f32([128, 2, NH, 2]); b_halo_ffn = [[nbuf(r_) for _ in range(NH)] for _ in range(2)]
    halo_g, r_ = AR.bf16([128, 8, 30]); b_halo_g = nbuf(r_)
    zcar, r_ = AR.f32([128, 2, 16]); b_zcar = nbuf(r_)
    STAGE0 = AR.mark()

    x32b = [[Buf("x32_%d_%d" % (k, t)) for t in range(3)] for k in range(8)]
    xbb = [[Buf("xb_%d_%d" % (k, t)) for t in range(3)] for k in range(8)]
    for k in range(8):
        for t in range(3):
            rX32[2].append(x32b[k][t])
            rXB[2].append(xbb[k][t])

    def colP(off):
        return pcol[:, off:off + 1]

    MEMSET("pool", ident, 0.0, [b_ident])
    kb.op("pool", lambda e: e.affine_select(ident, ident, pattern=[[-1, 128]], compare_op=ALU.not_equal, fill=1.0,
                                            base=0, channel_multiplier=1), [b_ident], [b_ident])
    CP("dve", identb, ident, [b_ident], [b_identb])
    MEMSET("dve", ones_c, 1.0 / 1024.0, [b_ones_c])
    MEMSET("dve", cst[:, 0:1], -math.pi, [b_cst])
    kb.dma("sp", pcol, D["pcol"], writes=[b_pcol])
    kb.dma("sp", convst, D["convst"], writes=[b_convst])
    TT("dve", m32f, ident[:, 0:32], ident[:, 32:64], ALU.add, [b_ident], [b_m32f])
    TT("dve", m32f, m32f, ident[:, 64:96], ALU.add, [b_ident, b_m32f], [b_m32f])
    TT("dve", m32f, m32f, ident[:, 96:128], ALU.add, [b_ident, b_m32f], [b_m32f])
    CP("dve", m32, m32f, [b_m32f], [b_m32])
    MEMSET("dve", halo_g, 0.0, [b_halo_g])
    for l in range(2):
        for j in range(NH):
            MEMSET("pool", halo_ffn[:, l, j, :], 0.0, [b_halo_ffn[l][j]])

    def load_w2(dram_ap, kt, ncols):
        i = slot_rr[0]
        slot_rr[0] = (i + 1) % 3
        ap, b = slots[i]
        view = ap[:, 0:kt * ncols].rearrange("p (k n) -> p k n", k=kt)
        src = dram_ap.rearrange("(k p) n -> p k n", p=128)
        chunks = []
        k0 = 0
        while k0 < kt:
            k1 = min(kt, k0 + 8)
            cb = Buf("wchunk")
            inherit(cb, [b] + slot_chunks[i])
            kb.dma("pool", view[:, k0:k1, :], src[:, k0:k1, :], writes=[cb])
            chunks.append(cb)
            k0 = k1
        slot_chunks[i] = chunks
        return view, chunks

    slot_chunks = [[], [], []]

    def load_wm(blocks, kt):
        i = slot_rr[0]
        slot_rr[0] = (i + 1) % 3
        ap, b = slots[i]
        tot = sum(nc_ for _, nc_ in blocks)
        view = ap[:, 0:kt * tot].rearrange("p (k n) -> p k n", k=kt)
        bufs = []
        off = 0
        olds = [b] + slot_chunks[i]
        for dram_ap, ncols in blocks:
            src = dram_ap.rearrange("(k p) n -> p k n", p=128)
            cb = Buf("wchunk")
            inherit(cb, olds)
            kb.dma("pool", view[:, :, off:off + ncols], src, writes=[cb])
            bufs.append(cb)
            off += ncols
        slot_chunks[i] = bufs
        return view, bufs

    def wb_of(chunks, k):
        return chunks[k // 8]

    def tiles_of(st):
        t = [(0, 512, 0), (512, 512, 1)]
        if st == 1:
            t.append((1024, NS, 2))
        return t

    def next_stg():
        i = stg_rr[0]
        stg_rr[0] = (i + 1) % 2
        return stg[i]

    def load_rows_T(dram_rows, R, C, dst_fn, dst_bufs_fn, stage=None):
        if stage is None:
            sap, sb_ = next_stg()
        else:
            sap, sb_ = stage
        kb.dma("sp", sap[:R, 0:C], dram_rows, writes=[sb_])
        if "in1" in stages:
            return
        nkb = C // 128
        per = max(1, min(512 // R, nkb))
        k0 = 0
        while k0 < nkb:
            nk = min(per, nkb - k0)
            bk, bb = bank()
            for j in range(nk):
                TR(bk[:, j * R:(j + 1) * R], sap[:R, (k0 + j) * 128:(k0 + j + 1) * 128], ident[:R, :R],
                   [sb_, b_ident], [bb])
            dsts = dst_fn(k0, nk)
            if "in2" in stages:
                dsts = []
            if "in3" in stages:
                dsts = dsts[:1]
            for (eng, dap, dbufs) in dsts:
                src = bk[:, 0:nk * R].rearrange("p (a b) -> p a b", a=nk)
                CP(eng, dap, src, [bb], dbufs)
            k0 += nk

    def store_rows_T(src_fn, src_bufs_fn, R, C, dram_rows, stage=None):
        if stage is None:
            sap, sb_ = next_stg()
        else:
            sap, sb_ = stage
        nkb = C // 128
        k0 = 0
        while k0 < nkb:
            nk = min(4, nkb - k0)
            bk, bb = bank()
            for j in range(nk):
                TR(bk[:R, j * 128:(j + 1) * 128], src_fn(k0 + j), ident, src_bufs_fn(k0 + j) + [b_ident], [bb])
            CP("act", sap[:R, k0 * 128:(k0 + nk) * 128], bk[:R, 0:nk * 128], [bb], [sb_])
            k0 += nk
        kb.dma("sp", dram_rows, sap[:R, 0:C], reads=[sb_])

    def ln_feat(n, nk, src, src_b, gcol, bcol, dst32, dst32_b, dstb, dstb_b, func=AF.Identity, ones=None,
                src4=None, dst32_4=None, dstb_4=None):
        ones = ones_c if ones is None else ones
        bm, bmb = bank()
        be, beb = bank()
        if src4 is not None:
            for k0 in range(0, nk, 4):
                sbs = [src_b(k) for k in range(k0, k0 + 4)]
                CP("act", ln_rb4[:, :, :n], src4(k0), sbs, [b_ln_rb4])
                for j in range(4):
                    k = k0 + j
                    MM(bm[:, :n], ones, ln_rb4[:, j, :n], k == 0, k == nk - 1, [b_ln_rb4, b_ones_c], [bmb])
                ACT(ln_sq4[:, :, :n], src4(k0), AF.Square, sbs, [b_ln_sq4])
                for j in range(4):
                    k = k0 + j
                    MM(be[:, :n], ones, ln_sq4[:, j, :n], k == 0, k == nk - 1, [b_ln_sq4, b_ones_c], [beb])
        else:
            raise AssertionError("grouped source view required")
        (m2, m2b), (rs, rsb) = ln_st
        ACT(m2[:, :n], bm[:, :n], AF.Square, [bmb], [m2b])
        TT("dve", m2[:, :n], be[:, :n], m2[:, :n], ALU.subtract, [beb, m2b], [m2b])
        TS("dve", m2[:, :n], m2[:, :n], 0.0, LN_EPS, ALU.max, ALU.add, [m2b], [m2b])
        ACT(m2[:, :n], m2[:, :n], AF.Ln, [m2b], [m2b])
        ACT(rs[:, :n], m2[:, :n], AF.Exp, [m2b], [rsb], scale=-0.5)
        for k in range(nk):
            i = ln_rr[0]
            ln_rr[0] = (i + 1) % 2
            (t1, t1b), (t2, t2b) = ln_t[i], ln_t2[i]
            TT("dve", t1[:, :n], src(k), bm[:, :n], ALU.subtract, [src_b(k), bmb], [t1b])
            TT("dve", t2[:, :n], t1[:, :n], rs[:, :n], ALU.mult, [t1b, rsb], [t2b])
            if dst32 is not None:
                ACT(dst32(k), t2[:, :n], AF.Identity, [t2b, b_pcol], [dst32_b(k)], bias=bcol(k), scale=gcol(k))
                if dstb is not None:
                    if dstb_4 is None:
                        CP("act", dstb(k), dst32(k), [dst32_b(k)], [dstb_b(k)])
                    elif k % 4 == 3:
                        CP("act", dstb_4(k - 3), dst32_4(k - 3), [dst32_b(kk) for kk in range(k - 3, k + 1)],
                           [dstb_b(kk) for kk in range(k - 3, k + 1)])
            else:
                ACT(dstb(k), t2[:, :n], func, [t2b, b_pcol], [dstb_b(k)], bias=bcol(k), scale=gcol(k))

    def residual_ln(st, l, which, tiles=None):
        g0 = PC[(which + "g", l)]
        b0 = PC[(which + "b", l)]
        last = (which == "ffn" and l == 1)
        for (c0, n, ti) in (tiles_of(st) if tiles is None else tiles):
            ln_feat(n, 8,
                    lambda k: X32[:, k, c0:c0 + n], lambda k: x32b[k][ti],
                    lambda k: colP(g0 + k), lambda k: colP(b0 + k),
                    lambda k: X32[:, k, c0:c0 + n], lambda k: x32b[k][ti],
                    None if last else (lambda k: XB[:, k, c0:c0 + n]), lambda k: xbb[k][ti],
                    src4=lambda k0: X32[:, k0:k0 + 4, c0:c0 + n], dst32_4=lambda k0: X32[:, k0:k0 + 4, c0:c0 + n],
                    dstb_4=lambda k0: XB[:, k0:k0 + 4, c0:c0 + n])

    def out_proj(st, wdram, ymf, ymbf, l, pre_ln=None):
        wblk = [load_w2(wdram[:, blk * 512:(blk + 1) * 512], 8, 512) for blk in range(2)]
        for (c0, n, ti) in tiles_of(st):
            for blk in range(2):
                wv, wch = wblk[blk]
                for m in range(4):
                    bk, bb = bank()
                    for k in range(8):
                        MM(bk[:, :n], wv[:, k, m * 128:(m + 1) * 128], ymf(k, c0, n), k == 0, k == 7,
                           [wb_of(wch, k), ymbf(k, ti)], [bb])
                    km = blk * 4 + m
                    STT(X32[:, km, c0:c0 + n], X32[:, km, c0:c0 + n], ALPHA, bk[:, :n], ALU.mult, ALU.add,
                        [bb, x32b[km][ti]], [x32b[km][ti]])
            if pre_ln is not None:
                pre_ln(ti)
            residual_ln(st, l, "mix", [(c0, n, ti)])

    def load_inputs(st, mode):
        def dsts_for(c, ti, w):
            def f(k0, nk):
                if mode == "b":
                    return [("act", XB[:, k0:k0 + nk, c:c + w], [xbb[k][ti] for k in range(k0, k0 + nk)])]
                return [("act", X32[:, k0:k0 + nk, c:c + w], [x32b[k][ti] for k in range(k0, k0 + nk)])]
            return f
        for rbk in range(8):
            r0 = st * 1024 + rbk * 128
            c = rbk * 128
            load_rows_T(D["xp"][r0:r0 + 128, :], 128, 1024, dsts_for(c, c // 512, 128), None)
        if st == 1:
            load_rows_T(D["xs"][:, :], NS, 1024, dsts_for(1024, 2, NS), None)

    def store_outputs(st):
        for rbk in range(8):
            r0 = st * 1024 + rbk * 128
            c = rbk * 128
            ti = c // 512
            store_rows_T(lambda k, c=c: X32[:, k, c:c + 128], lambda k, ti=ti: [x32b[k][ti]], 128, 1024,
                         D["yp"][r0:r0 + 128, :])
        if st == 1:
            store_rows_T(lambda k: X32[:, k, 1024:1024 + NS], lambda k: [x32b[k][2]], NS, 1024, D["ys"][:, :])

    def ffn(st, l, mid_hook=None):
        m0 = AR.mark()
        HID, rH = AR.bf16([128, NH, NCOL])
        hidb = [[nbuf(rH) for _ in range(3)] for _ in range(NH)]
        GRJ = [AR.f32([128, 2 + 1024]) for _ in range(6)]
        GRJ = [(ap_, [nbuf(r_), nbuf(r_), nbuf(r_)]) for ap_, r_ in GRJ]
        AA = [AR.f32([128, 512]) for _ in range(2)]
        AA = [(a_, nbuf(r_)) for a_, r_ in AA]
        SG = [AR.f32([128, 512]) for _ in range(2)]
        SG = [(a_, nbuf(r_)) for a_, r_ in SG]
        rr = [0]
        wg, wu, wd = D["ffn_w_gate"][l], D["ffn_w_up"][l], D["ffn_w_down"][l]
        fcw, fcb = PC[("fcw", l)], PC[("fcb", l)]
        if st == 1:
            CF, rCF = AR.f32([128, NH, NS * 2])
            b_CF = nbuf(rCF)
            NEWG, rNG = AR.f32([128, NH, NS])
            b_NEWG = nbuf(rNG)
            CFP, rCFP = AR.f32([128, NH, 2])
            b_CFP = nbuf(rCFP)
            bigst, rbs = AR.f32([128, DFF])
            b_bigst = nbuf(rbs)
            load_rows_T(D["cfc"][l], NS * 2, DFF,
                        lambda k0, nk: [("act", CF[:, k0:k0 + nk, :], [b_CF])], None, stage=(bigst, b_bigst))
        units = list(range(NH // 2))
        for g0 in range(0, len(units), 3):
            grp_units = units[g0:g0 + 3]
            loaded = {}
            for un in grp_units:
                hb0 = un * 2
                cs_ = slice(hb0 * 128, (hb0 + 2) * 128)
                loaded[un] = load_wm([(wg[:, cs_], 256), (wu[:, cs_], 256)], 8)
                for j in range(2):
                    hj = hb0 + j
                    grj, (gh_b, g0_b, g1_b) = GRJ[(un % 3) * 2 + j]
                    CP("act", grj[:, 0:2], halo_ffn[:, l, hj, :], [b_halo_ffn[l][hj]], [gh_b])
            for (c0, n, ti) in tiles_of(st):
                for un in grp_units:
                    hb0 = un * 2
                    sv, (gch, uch) = loaded[un]
                    for j in range(2):
                        hj = hb0 + j
                        grj, gbufs = GRJ[(un % 3) * 2 + j]
                        gb, gbb = bank()
                        for k in range(8):
                            MM(gb[:, :n], sv[:, k, j * 128:(j + 1) * 128], XB[:, k, c0:c0 + n], k == 0, k == 7,
                               [gch, xbb[k][ti]], [gbb])
                        ub, ubb = bank()
                        for k in range(8):
                            MM(ub[:, :n], sv[:, k, 256 + j * 128:256 + (j + 1) * 128], XB[:, k, c0:c0 + n], k == 0, k == 7,
                               [uch, xbb[k][ti]], [ubb])
                        i = rr[0]
                        rr[0] = (i + 1) % 2
                        (aa, aab), (sg, sgb) = AA[i], SG[i]
                        w0, w1, w2, bc = colP(fcw + hj), colP(fcw + NH + hj), colP(fcw + 2 * NH + hj), colP(fcb + hj)
                        if ti < 2:
                            cur = gbufs[1 + ti]
                            prev = gbufs[ti]
                            CP("act", grj[:, 2 + c0:2 + c0 + n], gb[:, :n], [gbb], [cur])
                            ACT(aa[:, :n], gb[:, :n], AF.Identity, [gbb, b_pcol], [aab], bias=bc, scale=w2)
                            STT(aa[:, :n], grj[:, 1 + c0:1 + c0 + n], w1, aa[:, :n], ALU.mult, ALU.add,
                                [cur, prev, aab, b_pcol], [aab])
                            STT(aa[:, :n], grj[:, c0:c0 + n], w0, aa[:, :n], ALU.mult, ALU.add, [cur, prev, aab, b_pcol],
                                [aab])
                            if ti == 1:
                                if st == 0:
                                    CP("act", halo_ffn[:, l, hj, :], grj[:, 1024:1026], [cur], [b_halo_ffn[l][hj]])
                                else:
                                    CP("act", CFP[:, hj, :], grj[:, 1024:1026], [cur], [b_CFP])
                        else:
                            cfv = CF[:, hj, :].rearrange("p (s k) -> p s k", k=2)
                            CP("act", NEWG[:, hj, :], gb[:, :n], [gbb], [b_NEWG])
                            ACT(aa[:, :n], gb[:, :n], AF.Identity, [gbb, b_pcol], [aab], bias=bc, scale=w2)
                            STT(aa[:, :n], cfv[:, :, 1], w1, aa[:, :n], ALU.mult, ALU.add, [b_CF, aab, b_pcol], [aab])
                            STT(aa[:, :n], cfv[:, :, 0], w0, aa[:, :n], ALU.mult, ALU.add, [b_CF, aab, b_pcol], [aab])
                        ACT(sg[:, :n], aa[:, :n], AF.Silu, [aab], [sgb])
                        TT("dve", HID[:, hj, c0:c0 + n], sg[:, :n], ub[:, :n], ALU.mult, [sgb, ubb], [hidb[hj][ti]])
        if st == 1:
            for kk in range(2):
                kb.dma("sp", D["cfp"][l][kk].rearrange("(j p) -> p j", p=128), CFP[:, :, kk], reads=[b_CFP],
                       allow_slow_non_contiguous=True)
            cfs_v = D["cfs"][l].rearrange("(s k) c -> s k c", k=2)
            cfc_v = D["cfc"][l].rearrange("(s k) c -> s k c", k=2)
            kb.dma("sp", cfs_v[:, 0, :], cfc_v[:, 1, :])
            store_rows_T(lambda k: NEWG[:, k, :], lambda k: [b_NEWG], NS, DFF, cfs_v[:, 1, :], stage=(bigst, b_bigst))
        if mid_hook is not None:
            mid_hook()
        tl = tiles_of(st)
        passes = [[tl[0]], tl[1:]]
        for ps_ in passes:
            for m in range(8):
                dv, dch = load_w2(wd[:, m * 128:(m + 1) * 128], NH, 128)
                for (c0, n, ti) in ps_:
                    bk, bb = bank()
                    for k in range(NH):
                        MM(bk[:, :n], dv[:, k, :], HID[:, k, c0:c0 + n], k == 0, k == NH - 1,
                           [wb_of(dch, k), hidb[k][ti]], [bb])
                    STT(X32[:, m, c0:c0 + n], X32[:, m, c0:c0 + n], ALPHA, bk[:, :n], ALU.mult, ALU.add,
                        [bb, x32b[m][ti]], [x32b[m][ti]])
            residual_ln(st, l, "ffn", ps_)
        AR.release(m0)

    def mixer_odd(st):
        m0 = AR.mark()
        W = D["w_in_c"]
        GB, rGB = AR.bf16([128, 8, 30 + NCOL])
        gbb_ = [[nbuf(rGB) for _ in range(3)] for _ in range(8)]
        b_gbh = [nbuf(rGB) for _ in range(8)]
        H32, rH = AR.f32([128, 8, NCOL])
        h32b = [[nbuf(rH) for _ in range(3)] for _ in range(8)]
        SGT = [AR.f32([128, 512]) for _ in range(2)]
        SGT = [(a, nbuf(r)) for a, r in SGT]
        LWs = [AR.bf16([128, 32, 32]) for _ in range(2)]
        LWs = [(a, nbuf(r)) for a, r in LWs]
        GSW = 1052
        GSs = []
        for _ in range(3):
            ap_, r_ = AR.bf16([128, 4, GSW])
            bufs_ = [[nbuf(r_) for _j in range(4)] for _q in range(4)]
            MEMSET("dve", ap_, 0.0, [b_ for row in bufs_ for b_ in row])
            GSs.append((ap_, bufs_))
        rr = [0]
        m_tmp = AR.mark()
        if st == 1:
            G32T, r_ = AR.f32([128, 8, 30]); b_G32T = nbuf(r_)
            GS32, r_ = AR.f32([128, 8, NS]); b_GS32 = nbuf(r_)
            CC, r_ = AR.f32([128, 8, NS * 30]); b_CC = nbuf(r_)
            CT, r_ = AR.f32([128, NS * 30]); b_CT = nbuf(r_)
            RED, r_ = AR.f32([128, NS]); b_RED = nbuf(r_)
            for q in range(4):
                load_rows_T(D["ccc"][q * 120:(q + 1) * 120, :], 120, 1024,
                            lambda k0, nk, q=q: [("act", CC[:, k0:k0 + nk, q * 120:(q + 1) * 120], [b_CC])], None)
        for ct in range(8):
            CP("act", GB[:, ct, 0:30], halo_g[:, ct, :], [b_halo_g], [b_gbh[ct]])
        def emit_gs(ct):
            if ct >= 8:
                return
            gs, gsb = GSs[ct % 3]
            gsd, gsdb = GSD[ct % 3]
            kb.dma("sp", gsd[:, 0:1054], GB[:, ct, 0:1054], reads=[b_gbh[ct], gbb_[ct][0], gbb_[ct][1]], writes=[gsdb])
            for j in range(4):
                wd_ = min(GSW, 1054 - j)
                kb.dma("sp", gs[32 * j:32 * j + 32, :, 0:wd_],
                       gsd[:, j:j + wd_].rearrange("(q c) x -> c q x", c=32),
                       reads=[gsdb], writes=[gsb[q_][j] for q_ in range(4)])

        for un in range(4):
            if un == 1:
                emit_gs(0)
                emit_gs(1)
            if un == 2:
                emit_gs(2)
            sv, (ach, bch) = load_wm([(W[:, un * 256:(un + 1) * 256], 256), (W[:, 1024 + un * 256:1024 + (un + 1) * 256], 256)], 8)
            for (c0, n, ti) in tiles_of(st):
                for j in range(2):
                    ct = un * 2 + j
                    bb_, bbb = bank()
                    ab, abb = bank()
                    for k in range(8):
                        MM(bb_[:, :n], sv[:, k, 256 + j * 128:256 + (j + 1) * 128], XB[:, k, c0:c0 + n], k == 0, k == 7,
                           [bch, xbb[k][ti]], [bbb])
                    for k in range(8):
                        MM(ab[:, :n], sv[:, k, j * 128:(j + 1) * 128], XB[:, k, c0:c0 + n], k == 0, k == 7,
                           [ach, xbb[k][ti]], [abb])
                    i = rr[0]
                    rr[0] = (i + 1) % 2
                    sgt, sgtb = SGT[i]
                    ACT(sgt[:, :n], bb_[:, :n], AF.Sigmoid, [bbb], [sgtb])
                    if ti < 2:
                        TT("dve", GB[:, ct, 30 + c0:30 + c0 + n], ab[:, :n], sgt[:, :n], ALU.mult, [abb, sgtb],
                           [gbb_[ct][ti]])
                        if st == 1 and ti == 1:
                            TT("dve", G32T[:, ct, :], ab[:, n - 30:n], sgt[:, n - 30:n], ALU.mult, [abb, sgtb],
                               [b_G32T])
                    else:
                        TT("dve", GS32[:, ct, :], ab[:, :n], sgt[:, :n], ALU.mult, [abb, sgtb], [b_GS32])
        if st == 0:
            for ct in range(8):
                CP("act", halo_g[:, ct, :], GB[:, ct, 1024:1054], [gbb_[ct][1]], [b_halo_g])
        cw = PC["convcw"]
        if st == 1:
            c0, n, ti = tiles_of(st)[2]
            for ct in range(8):
                ccv = CC[:, ct, :].rearrange("p (s k) -> p s k", k=30)
                wrow = pcol[:, cw + ct:cw + ct + 30 * 8:8]
                TT("dve", CT.rearrange("p (s k) -> p s k", k=30), ccv, wrow.unsqueeze(1).to_broadcast([128, NS, 30]),
                   ALU.mult, [b_CC, b_pcol], [b_CT])
                kb.op("dve", lambda e: e.tensor_reduce(RED, CT.rearrange("p (s k) -> p s k", k=30), AX.X, ALU.add),
                      [b_CT], [b_RED])
                STT(RED, GS32[:, ct, :], colP(cw + 30 * 8 + ct), RED, ALU.mult, ALU.add, [b_GS32, b_RED, b_pcol],
                    [b_RED])
                ACT(H32[:, ct, c0:c0 + n], RED, AF.Identity, [b_RED, b_pcol], [h32b[ct][ti]],
                    bias=colP(PC["convcb"] + ct))
            store_rows_T(lambda k: G32T[:, k, :], lambda k: [b_G32T], 30, 1024, D["ccp"][:, :])
            ccs_v = D["ccs"].rearrange("(s k) c -> s k c", k=30)
            ccc_v = D["ccc"].rearrange("(s k) c -> s k c", k=30)
            kb.dma("sp", ccs_v[:, 0:29, :], ccc_v[:, 1:30, :])
            store_rows_T(lambda k: GS32[:, k, :], lambda k: [b_GS32], NS, 1024, ccs_v[:, 29, :])
        AR.release(m_tmp)
        HB, rHB = AR.bf16([128, 8, NCOL])
        hbb = [[nbuf(rHB) for _ in range(3)] for _ in range(8)]
        wblk = [load_w2(D["w_out_c"][:, blk * 512:(blk + 1) * 512], 8, 512) for blk in range(2)]

        def part_A(c0, n, ti):
            ln_feat(n, 8, lambda k: H32[:, k, c0:c0 + n], lambda k: h32b[k][ti],
                    lambda k: colP(PC["lncg"] + k), lambda k: colP(PC["lncb"] + k),
                    None, None, lambda k: HB[:, k, c0:c0 + n], lambda k: hbb[k][ti], func=AF.Silu,
                    src4=lambda k0: H32[:, k0:k0 + 4, c0:c0 + n])

        def part_B(c0, n, ti):
            for blk in range(2):
                wv, wch = wblk[blk]
                for m in range(4):
                    bk, bb = bank()
                    for k in range(8):
                        MM(bk[:, :n], wv[:, k, m * 128:(m + 1) * 128], HB[:, k, c0:c0 + n], k == 0, k == 7,
                           [wb_of(wch, k), hbb[k][ti]], [bb])
                    km = blk * 4 + m
                    STT(X32[:, km, c0:c0 + n], X32[:, km, c0:c0 + n], ALPHA, bk[:, :n], ALU.mult, ALU.add,
                        [bb, x32b[km][ti]], [x32b[km][ti]])

        def part_C(c0, n, ti):
            residual_ln(st, 1, "mix", [(c0, n, ti)])

        tl = tiles_of(st)
        t0_, t1_ = tl[0], tl[1]
        for ct in range(8):
            i = rr[0]
            rr[0] = (i + 1) % 2
            lw, lwb = LWs[i]
            gs, gsb = GSs[ct % 3]
            TT("dve", lw, m32.unsqueeze(1).to_broadcast([128, 32, 32]),
               convst[:, ct * 32:(ct + 1) * 32].unsqueeze(2).to_broadcast([128, 32, 32]), ALU.mult,
               [b_m32, b_convst], [lwb])
            for (c0, n, ti) in (t0_, t1_):
                bk, bb = bank()
                for tg in range(8):
                    for q in range(4):
                        kb.op("pe", lambda e, bk=bk, q=q, tg=tg, lw=lw, gs=gs, c0=c0, n=n: e.matmul(
                            bk[32 * q:32 * q + 32, :n], lw[:, q * 8 + tg, :], gs[:, q, c0 + 4 * tg:c0 + 4 * tg + n],
                            start=(tg == 0), stop=(tg == 7), tile_position=(0, 32 * q)),
                            [lwb] + gsb[q], [bb])
                ACT(H32[:, ct, c0:c0 + n], bk[:, :n], AF.Identity, [bb, b_pcol], [h32b[ct][ti]],
                    bias=colP(PC["convcb"] + ct))
            emit_gs(ct + 3)
        part_A(*t0_)
        part_A(*t1_)
        part_B(*t0_)
        part_C(*t0_)
        if st == 1:
            part_A(*tl[2])
        part_B(*t1_)
        part_C(*t1_)
        if st == 1:
            part_B(*tl[2])
            part_C(*tl[2])
        AR.release(m0)

    def mixer_even(st, pre_s5_hook=None):
        m0 = AR.mark()
        W = D["w_in_ab"]
        nchunk = 8
        YM, rYM = AR.bf16([128, 8, NCOL])
        ymb = [[nbuf(rYM) for _ in range(3)] for _ in range(8)]
        UAB, rUA = AR.bf16([128, 4, NCOL])
        uab = [[nbuf(rUA) for _ in range(3)] for _ in range(4)]
        r_COS0 = AR.mark()
        COS, r_ = AR.f32([128, 16, LCH]); b_COS = nbuf(r_)
        SIN, r_ = AR.f32([128, 16, LCH]); b_SIN = nbuf(r_)
        RT, r_ = AR.f32([128, 16, LCH]); b_RT = nbuf(r_)
        SM, r_ = AR.f32([128, 24, 16]); b_SM = nbuf(r_)
        cw_m = AR.mark()
        BTR, r_ = AR.bf16([128, 16, 128]); b_BTR = nbuf(r_)
        BTI, r_ = AR.bf16([128, 16, 128]); b_BTI = nbuf(r_)
        CR, r_ = AR.bf16([128, 16, 128]); b_CR = nbuf(r_)
        NCR, r_ = AR.bf16([128, 16, 128]); b_NCR = nbuf(r_)
        NCI, r_ = AR.bf16([128, 16, 128]); b_NCI = nbuf(r_)
        DD, r_ = AR.bf16([128, 4, 128]); b_DD = nbuf(r_)
        (S_DT, S_A, S_TH, S_R, S_F, S_LBR, S_LBI, S_CLR, S_SLR, S_KRE, S_KIM, S_T0, S_T1, S_T2, S_T3, S_LRE, S_LIM,
         S_CL1, S_SL1) = range(19)

        def sm(i):
            return SM[:, i, :]
        const_bufs = [b_COS, b_SIN, b_RT, b_BTR, b_BTI, b_CR, b_NCR, b_NCI, b_DD, b_SM]
        cf32_bufs = [b_COS, b_SIN, b_RT, b_SM]
        cbf_bufs = [b_BTR, b_BTI, b_CR, b_NCR, b_NCI, b_DD]
        cw_a, cw_b = r_COS0, AR.mark()
        def s5_setup_compute():
            MEMSET("dve", AR_t[:, cw_a:cw_b], 0.0, const_bufs)
            m2 = AR.mark()
            LR, r_ = AR.f32([128, 16]); b_LR = nbuf(r_)
            LI, r_ = AR.f32([128, 16]); b_LI = nbuf(r_)
            LD, r_ = AR.f32([128, 16]); b_LD = nbuf(r_)
            TAUI, r_ = AR.f32([128, LCH]); b_TAUI = nbuf(r_)
            TAU, r_ = AR.f32([128, LCH]); b_TAU = nbuf(r_)
            XR = rX32[0]

            def xr_buf(k_):
                r__ = AR.at(XR + 2048 * k_, 2048)
                return AR_t[:, XR + 2048 * k_:XR + 2048 * (k_ + 1)].rearrange("p (a b) -> p a b", a=16), nbuf(r__)
            PH, b_PH = xr_buf(0)
            BX1, b_BX1 = xr_buf(1)
            BX2, b_BX2 = xr_buf(2)
            BT2, b_BT2 = xr_buf(3)
            BT1, b_BT1 = PH, b_PH
            kb.dma("sp", LR, D["lamre"], writes=[b_LR])
            kb.dma("sp", LI, D["lamim"], writes=[b_LI])
            kb.dma("sp", LD, D["logdt"], writes=[b_LD])
            kb.dma("pool", CR, D["crex"], writes=[b_CR])
            kb.dma("pool", NCI, D["cimx"], writes=[b_NCI])
            TS("dve", NCR, CR, -1.0, None, ALU.mult, None, [b_CR], [b_NCR])
            TS("dve", NCI, NCI, -1.0, None, ALU.mult, None, [b_NCI], [b_NCI])
            for ct in range(4):
                TS("dve", DD[:, ct, :], identb, colP(PC["s5d"] + ct), None, ALU.mult, None, [b_identb, b_pcol], [b_DD])
            S = [b_SM]
            ACT(sm(S_DT), LD, AF.Exp, [b_LD], S)
            TT("dve", sm(S_A), LR, sm(S_DT), ALU.mult, [b_LR] + S, S)
            TT("dve", sm(S_TH), LI, sm(S_DT), ALU.mult, [b_LI] + S, S)
            ACT(sm(S_R), sm(S_A), AF.Exp, S, S)
            SMI, r_ = AR.f32([128, 16]); b_SMI = nbuf(r_)

            def frac_sym(x, xb_, ti, tib, tf, tfb):
                CP("dve", ti, x, [xb_], [tib])
                CP("dve", tf, ti, [tib], [tfb])
                TT("dve", x, x, tf, ALU.subtract, [xb_, tfb], [xb_])

            def sincos(dst_sin, dsb, dst_cos, dcb, ph, phb, ti, tib, tf, tfb):
                frac_sym(ph, phb, ti, tib, tf, tfb)
                ACT(dst_sin, ph, AF.Sin, [phb], [dsb], scale=2.0 * math.pi)
                TS("dve", ph, ph, 0.25, None, ALU.add, None, [phb], [phb])
                frac_sym(ph, phb, ti, tib, tf, tfb)
                ACT(dst_cos, ph, AF.Sin, [phb], [dcb], scale=2.0 * math.pi)
            TS("dve", sm(S_F), sm(S_TH), 1.0 / (2.0 * math.pi), None, ALU.mult, None, S, S)
            frac_sym(sm(S_F), b_SM, SMI.bitcast(I32), b_SMI, sm(S_T0), b_SM)
            kb.op("pool", lambda e: e.iota(TAUI.bitcast(I32), pattern=[[1, LCH]], base=0, channel_multiplier=0), (), [b_TAUI])
            CP("dve", TAU, TAUI.bitcast(I32), [b_TAUI], [b_TAU])
            TT("dve", PH, sm(S_F).unsqueeze(2).to_broadcast([128, 16, LCH]), TAU.unsqueeze(1).to_broadcast([128, 16, LCH]),
               ALU.mult, S + [b_TAU], [b_PH])
            sincos(SIN, b_SIN, COS, b_COS, PH, b_PH, BX1.bitcast(I32), b_BX1, BX2, b_BX2)
            TS("dve", sm(S_T0), sm(S_F), float(LCH), None, ALU.mult, None, S, S)
            sincos(sm(S_SLR), b_SM, sm(S_CLR), b_SM, sm(S_T0), b_SM, SMI.bitcast(I32), b_SMI, sm(S_T1), b_SM)
            TT("dve", sm(S_CLR), sm(S_CLR), sm(S_R), ALU.mult, S, S)
            TT("dve", sm(S_SLR), sm(S_SLR), sm(S_R), ALU.mult, S, S)
            TT("dve", sm(S_LBR), COS[:, :, 1], sm(S_R), ALU.mult, S + [b_COS], S)
            TT("dve", sm(S_LBI), SIN[:, :, 1], sm(S_R), ALU.mult, S + [b_SIN], S)
            CP("dve", sm(S_CL1), COS[:, :, LCH - 1], [b_COS], S)
            CP("dve", sm(S_SL1), SIN[:, :, LCH - 1], [b_SIN], S)
            TS("dve", sm(S_T0), sm(S_LBR), -1.0, None, ALU.add, None, S, S)
            TT("dve", sm(S_T1), LR, LR, ALU.mult, [b_LR], S)
            TT("dve", sm(S_T2), LI, LI, ALU.mult, [b_LI], S)
            TT("dve", sm(S_T1), sm(S_T1), sm(S_T2), ALU.add, S, S)
            kb.op("dve", lambda e: e.reciprocal(sm(S_T1), sm(S_T1)), S, S)
            TT("dve", sm(S_T2), sm(S_T0), LR, ALU.mult, S + [b_LR], S)
            TT("dve", sm(S_T3), sm(S_LBI), LI, ALU.mult, S + [b_LI], S)
            TT("dve", sm(S_T2), sm(S_T2), sm(S_T3), ALU.add, S, S)
            TT("dve", sm(S_KRE), sm(S_T2), sm(S_T1), ALU.mult, S, S)
            TT("dve", sm(S_T2), sm(S_LBI), LR, ALU.mult, S + [b_LR], S)
            TT("dve", sm(S_T3), sm(S_T0), LI, ALU.mult, S + [b_LI], S)
            TT("dve", sm(S_T2), sm(S_T2), sm(S_T3), ALU.subtract, S, S)
            TT("dve", sm(S_KIM), sm(S_T2), sm(S_T1), ALU.mult, S, S)
            CP("dve", RT, sm(S_R).unsqueeze(2).to_broadcast([128, 16, LCH]), S, [b_RT])
            MEMSET("dve", RT[:, :, 0:1], 0.0, [b_RT])
            kb.dma("sp", BX1, D["brex"], writes=[b_BX1])
            kb.dma("sp", BX2, D["bimx"], writes=[b_BX2])
            kre_b = sm(S_KRE).unsqueeze(2).to_broadcast([128, 16, 128])
            kim_b = sm(S_KIM).unsqueeze(2).to_broadcast([128, 16, 128])
            TT("dve", BT1, BX1, kre_b, ALU.mult, [b_BX1] + S, [b_BT1])
            TT("dve", BT2, BX2, kim_b, ALU.mult, [b_BX2] + S, [b_BT2])
            TT("dve", BT1, BT1, BT2, ALU.subtract, [b_BT1, b_BT2], [b_BT1])
            TT("dve", BT2, BX2, kre_b, ALU.mult, [b_BX2] + S, [b_BT2])
            TT("dve", BX1, BX1, kim_b, ALU.mult, [b_BX1] + S, [b_BX1])
            TT("dve", BT2, BT2, BX1, ALU.add, [b_BT2, b_BX1], [b_BT2])
            for (src, sb_, dst, db) in ((BT1, b_BT1, BTR, b_BTR), (BT2, b_BT2, BTI, b_BTI)):
                for p0 in range(0, 16, 4):
                    bk, bb = bank()
                    for j in range(4):
                        TR(bk[:, j * 128:(j + 1) * 128], src[:, p0 + j, :], ident, [sb_, b_ident], [bb])
                    CP("act", dst[:, p0:p0 + 4, :], bk.rearrange("p (a b) -> p a b", a=4), [bb], [db])
            AR.release(m2)
            kb.dma("sp", S5C[:, 0:cw_m - cw_a], AR_t[:, cw_a:cw_m], reads=cf32_bufs, writes=[b_S5C])
            kb.dma("sp", S5CB[:, 0:2 * (cw_b - cw_m)], AR_t[:, cw_m:cw_b].bitcast(BF16), reads=cbf_bufs, writes=[b_S5CB])
        if st == 1:
            kb.dma("sp", AR_t[:, cw_a:cw_m], S5C[:, 0:cw_m - cw_a], reads=[b_S5C], writes=cf32_bufs)
            kb.dma("sp", AR_t[:, cw_m:cw_b].bitcast(BF16), S5CB[:, 0:2 * (cw_b - cw_m)], reads=[b_S5CB], writes=cbf_bufs)
        m1 = AR.mark()
        VNT, rV = AR.bf16([128, nchunk + 1, 512])
        vntb = [nbuf(rV) for _ in range(nchunk + 1)]
        LNG, r_ = AR.f32([128, 512]); b_LNG = nbuf(r_)
        LNB, r_ = AR.f32([128, 512]); b_LNB = nbuf(r_)
        WST32, r_ = AR.f32([128, 4, 128]); b_WST32 = nbuf(r_)
        WST, r_ = AR.bf16([128, 4, 128]); b_WST = nbuf(r_)
        SGB, r_ = AR.f32([128, 4, 128]); b_SGB = nbuf(r_)
        VT = [AR.f32([128, 512]) for _ in range(1)] * 2
        VT = [(a, nbuf(r)) for a, r in VT[:1]] * 2
        VN32 = [AR.f32([128, 512]) for _ in range(1)]
        VN32 = [(a, nbuf(r)) for a, r in VN32] * 2
        BNS, r_ = AR.f32([128, 16]); b_BNS = nbuf(r_)
        GATE = [AR.f32([128, 512]) for _ in range(1)]
        GATE = [(a, nbuf(r)) for a, r in GATE] * 2
        kb.dma("sp", LNG, D["sglng"], writes=[b_LNG])
        kb.dma("sp", LNB, D["sglnb"], writes=[b_LNB])
        kb.dma("sp", WST32, D["sguwT"], writes=[b_WST32])
        kb.dma("sp", SGB, D["sgubc"], writes=[b_SGB])
        for h in range(4):
            kb.op("pool", lambda e, h=h: e.affine_select(WST32[:, h, :], WST32[:, h, :], pattern=[[1, 128]],
                                                         compare_op=ALU.is_ge, fill=0.0, base=0, channel_multiplier=-1),
                  [b_WST32], [b_WST32])
        CP("dve", WST, WST32, [b_WST32], [b_WST])
        if st == 1:
            W00, r_ = AR.f32([NS, 512]); b_W00 = nbuf(r_)
            B0, r_ = AR.f32([NS, 512]); b_B0 = nbuf(r_)
            GTOK, r_ = AR.f32([NS, 512]); b_GTOK = nbuf(r_)
            GSS, r_ = AR.f32([128, 4, NS]); b_GSS = nbuf(r_)
            kb.dma("sp", W00, D["sgw00"], writes=[b_W00])
            kb.dma("sp", B0, D["sgb0"], writes=[b_B0])
        vv, vch = load_w2(W[:, 1024:1536], 8, 512)
        rr = [0]
        chunks = [(c * 128, 128, c, c // 4) for c in range(nchunk)]
        if st == 1:
            chunks.append((1024, NS, nchunk, 2))
        SD = 6
        for (c0, M, ci, ti) in chunks:
            bk, bb = bank()
            for k in range(8):
                MM(bk[:M, :], XB[:, k, c0:c0 + M], vv[:, k, :], k == 0, k == 7, [wb_of(vch, k), xbb[k][ti]], [bb])
            i = rr[0]
            rr[0] = (i + 1) % 2
            (vt, vtb), (vn, vnb) = VT[i], VN32[i]
            kb.op("dve", lambda e, bk=bk, M=M: e.bn_stats(BNS[:M, 0:SD], bk[:M, :]), [bb], [b_BNS])
            kb.op("dve", lambda e, M=M: e.bn_aggr(BNS[:M, 8:10], BNS[:M, 0:SD]), [b_BNS], [b_BNS])
            TS("dve", BNS[:M, 10:11], BNS[:M, 9:10], 0.0, LN_EPS, ALU.max, ALU.add, [b_BNS], [b_BNS])
            ACT(BNS[:M, 10:11], BNS[:M, 10:11], AF.Ln, [b_BNS], [b_BNS])
            ACT(BNS[:M, 11:12], BNS[:M, 10:11], AF.Exp, [b_BNS], [b_BNS], scale=-0.5)
            TS("dve", vt[:M, :], bk[:M, :], BNS[:M, 8:9], BNS[:M, 11:12], ALU.subtract, ALU.mult, [bb, b_BNS], [vtb])
            TT("dve", vt[:M, :], vt[:M, :], LNG[:M, :], ALU.mult, [vtb, b_LNG], [vtb])
            TT("dve", vn[:M, :], vt[:M, :], LNB[:M, :], ALU.add, [vtb, b_LNB], [vnb])
            CP("act", VNT[:M, ci, :], vn[:M, :], [vnb], [vntb[ci]])
            if ci == nchunk:
                kb.dma("sp", D["sbv"][:, :], vn[:M, :], reads=[vnb])
                TT("dve", GTOK, vn[:M, :], W00, ALU.mult, [vnb, b_W00], [b_GTOK])
                TT("dve", GTOK, GTOK, B0, ALU.add, [b_GTOK, b_B0], [b_GTOK])
                bk2, bb2 = bank()
                for h in range(4):
                    TR(bk2[:, h * NS:(h + 1) * NS], GTOK[:NS, h * 128:(h + 1) * 128], ident[:NS, :NS],
                       [b_GTOK, b_ident], [bb2])
                CP("act", GSS, bk2[:, 0:4 * NS].rearrange("p (a b) -> p a b", a=4), [bb2], [b_GSS])
        uv, uch = load_w2(W[:, 512:1024], 8, 512)
        for (c0, n, ti) in tiles_of(st):
            for h in range(4):
                ub, ubb = bank()
                for k in range(8):
                    MM(ub[:, :n], uv[:, k, h * 128:(h + 1) * 128], XB[:, k, c0:c0 + n], k == 0, k == 7,
                       [wb_of(uch, k), xbb[k][ti]], [ubb])
                if ti < 2:
                    gk, gkb = bank()
                    for c in range(4):
                        ci = ti * 4 + c
                        MM(gk[:, c * 128:(c + 1) * 128], VNT[:, ci, h * 128:(h + 1) * 128], WST[:, h, :], True, True,
                           [vntb[ci], b_WST], [gkb])
                    i = rr[0]
                    rr[0] = (i + 1) % 2
                    ga, gab = GATE[i]
                    TT("dve", ga.rearrange("p (a b) -> p a b", a=4), gk.rearrange("p (a b) -> p a b", a=4),
                       SGB[:, h, :].unsqueeze(1).to_broadcast([128, 4, 128]), ALU.add, [gkb, b_SGB], [gab])
                    TT("dve", YM[:, 4 + h, c0:c0 + n], ub[:, :n], ga[:, :n], ALU.mult, [ubb, gab], [ymb[4 + h][ti]])
                else:
                    TT("dve", YM[:, 4 + h, c0:c0 + n], ub[:, :n], GSS[:, h, :], ALU.mult, [ubb, b_GSS], [ymb[4 + h][ti]])
        av, ach = load_w2(W[:, 0:512], 8, 512)
        for (c0, n, ti) in tiles_of(st):
            for m in range(4):
                bk, bb = bank()
                for k in range(8):
                    MM(bk[:, :n], av[:, k, m * 128:(m + 1) * 128], XB[:, k, c0:c0 + n], k == 0, k == 7,
                       [wb_of(ach, k), xbb[k][ti]], [bb])
                CP("act", UAB[:, m, c0:c0 + n], bk[:, :n], [bb], [uab[m][ti]])
        AR.release(m1)
        if st == 0:
            s5_setup_compute()
        if pre_s5_hook is not None:
            pre_s5_hook()
        glv, glch = load_w2(D["s5_glu_w"], 4, 512)
        X32_WORD0 = rX32[0]
        WZW = 2 * 16 * LCH
        rW0 = AR.at(XB_WORD0, XB_WORDS)
        rW1 = AR.at(X32_WORD0, WZW)
        WZs = [AR_t[:, XB_WORD0:XB_WORD0 + WZW].rearrange("p (c a b) -> p c a b", c=2, a=16),
               AR_t[:, X32_WORD0:X32_WORD0 + WZW].rearrange("p (c a b) -> p c a b", c=2, a=16)]
        b_Ws = [[nbuf(rW0), nbuf(rW0)], [nbuf(rW1), nbuf(rW1)]]
        xoff = [X32_WORD0 + WZW]
        scratch_bufs = [b_Ws[1][0], b_Ws[1][1]]

        def xalloc(words, shape, dt):
            w0_ = xoff[0]
            xoff[0] += (words + 7) // 8 * 8
            assert xoff[0] <= rX32[1]
            rec = AR.at(w0_, words)
            ap_ = AR_t[:, w0_:w0_ + words]
            if dt == BF16:
                ap_ = ap_.bitcast(BF16)
            bb_ = nbuf(rec)
            scratch_bufs.append(bb_)
            return Arena._shape(ap_, shape), bb_
        TM = [AR.f32([128, 512]) for _ in range(8)]
        TM = [(a_, nbuf(r_)) for a_, r_ in TM]
        ADD_ENG = "pool"
        def aalloc(shape, dt):
            ap_, r_ = (AR.bf16(shape) if dt == BF16 else AR.f32(shape))
            return ap_, nbuf(r_)
        PP = [aalloc([128, 4, 512], BF16) for _ in range(2)] + [xalloc(1024, [128, 4, 512], BF16) for _ in range(2)]
        PPK = []
        for (pp_, ppb_) in PP:
            kbufs = [Buf("ppk") for _ in range(4)]
            for kb_ in kbufs:
                inherit(kb_, [ppb_])
                scratch_bufs.append(kb_)
            PPK.append(kbufs)
        DEMOD_POOL = set()
        GAs = [(aalloc([128, 4, 128], F32), aalloc([128, 4, 128], BF16), aalloc([128, 4, 128], F32)),
               (xalloc(512, [128, 4, 128], F32), xalloc(256, [128, 4, 128], BF16), xalloc(512, [128, 4, 128], F32))]
        ZL, r_ = AR.f32([128, 4, 16]); b_ZL = nbuf(r_)
        prr = [0]

        def glu_front(ybank, ybb, n, gset, zbank=None):
            (GA32, b_GA32), (GAB, b_GAB), (SGG, b_SGG) = gset
            yv = ybank[:, 0:4 * n].rearrange("p (a b) -> p a b", a=4)
            ACT(GA32[:, :, :n], yv, AF.Gelu_apprx_tanh, [ybb], [b_GA32])
            CP("act", GAB[:, :, :n], GA32[:, :, :n], [b_GA32], [b_GAB])
            if zbank is None:
                zb, zbb = bank()
            else:
                zb, zbb = banks[zbank][:, :], bank_buf[zbank]
            for m in range(4):
                for k in range(4):
                    MM(zb[:, m * n:(m + 1) * n], glv[:, k, m * 128:(m + 1) * 128], GAB[:, k, :n], k == 0, k == 3,
                       [wb_of(glch, k), b_GAB], [zbb])
            for m in range(4):
                ACT(SGG[:, m, :n], zb[:, m * n:(m + 1) * n], AF.Sigmoid, [zbb, b_pcol], [b_SGG],
                    bias=colP(PC["glub"] + m))

        def glu_back(c0, n, ti, gset):
            (GA32, b_GA32), (GAB, b_GAB), (SGG, b_SGG) = gset
            for m in range(4):
                TT("dve", YM[:, m, c0:c0 + n], GA32[:, m, :n], SGG[:, m, :n], ALU.mult, [b_GA32, b_SGG], [ymb[m][ti]])

        def emit_bu(job):
            if job >= nchunk * 4:
                return
            c, grp = divmod(job, 4)
            c0 = c * LCH
            ti = c // 4
            are, areb = banks[(job % 2) * 2][:, :], bank_buf[(job % 2) * 2]
            aim, aimb = banks[(job % 2) * 2 + 1][:, :], bank_buf[(job % 2) * 2 + 1]
            for j in range(4):
                p = grp * 4 + j
                MM(are[:, j * 128:(j + 1) * 128], BTR[:, p, :], UAB[:, p // 4, c0:c0 + LCH], True, True,
                   [b_BTR, uab[p // 4][ti]], [areb])
                MM(aim[:, j * 128:(j + 1) * 128], BTI[:, p, :], UAB[:, p // 4, c0:c0 + LCH], True, True,
                   [b_BTI, uab[p // 4][ti]], [aimb])

        def stage_M(c):
            WZ, b_W = WZs[c % 2], b_Ws[c % 2]
            c0 = c * LCH
            ti = c // 4
            first = (st == 0 and c == 0)
            for grp in range(4):
                job = c * 4 + grp
                are, areb = banks[(job % 2) * 2][:, :], bank_buf[(job % 2) * 2]
                aim, aimb = banks[(job % 2) * 2 + 1][:, :], bank_buf[(job % 2) * 2 + 1]
                cs = COS[:, grp * 4:(grp + 1) * 4, :].rearrange("p a b -> p (a b)")
                sn = SIN[:, grp * 4:(grp + 1) * 4, :].rearrange("p a b -> p (a b)")
                (t1, t1b), (t2, t2b), (t3, t3b), (t4, t4b) = TM[(job % 2) * 4:(job % 2) * 4 + 4]
                wre = WZ[:, 0, grp * 4:(grp + 1) * 4, :].rearrange("p a b -> p (a b)")
                wim = WZ[:, 1, grp * 4:(grp + 1) * 4, :].rearrange("p a b -> p (a b)")
                TT("dve", t1, are, cs, ALU.mult, [areb, b_COS], [t1b])
                TT("dve", t2, aim, sn, ALU.mult, [aimb, b_SIN], [t2b])
                TT("dve", t3, aim, cs, ALU.mult, [aimb, b_COS], [t3b])
                TT("dve", t4, are, sn, ALU.mult, [areb, b_SIN], [t4b])
                emit_bu(job + 2)
                TT(ADD_ENG, wre, t1, t2, ALU.add, [t1b, t2b], [b_W[0]])
                TT(ADD_ENG, wim, t3, t4, ALU.subtract, [t3b, t4b], [b_W[1]])
            if not first:
                TT("dve", WZ[:, 0, :, 0], WZ[:, 0, :, 0], zcar[:, 0, :], ALU.add, [b_W[0], b_zcar], [b_W[0]])
                TT("dve", WZ[:, 1, :, 0], WZ[:, 1, :, 0], zcar[:, 1, :], ALU.add, [b_W[1], b_zcar], [b_W[1]])
            for ri in range(2):
                flat = WZ[:, ri, :, :].rearrange("p a b -> p (a b)")
                kb.op("dve", lambda e, flat=flat: e.tensor_tensor_scan(flat, RT.rearrange("p a b -> p (a b)"), flat, 0.0,
                                                                       ALU.mult, ALU.add),
                      [b_RT, b_W[ri]], [b_W[ri]])
            zlr, zli = WZ[:, 0, :, LCH - 1], WZ[:, 1, :, LCH - 1]
            TT("dve", ZL[:, 0, :], zlr, sm(S_CLR), ALU.mult, [b_W[0], b_SM], [b_ZL])
            TT("dve", ZL[:, 1, :], zli, sm(S_SLR), ALU.mult, [b_W[1], b_SM], [b_ZL])
            TT("dve", ZL[:, 2, :], zlr, sm(S_SLR), ALU.mult, [b_W[0], b_SM], [b_ZL])
            TT("dve", ZL[:, 3, :], zli, sm(S_CLR), ALU.mult, [b_W[1], b_SM], [b_ZL])
            TT("dve", zcar[:, 0, :], ZL[:, 0, :], ZL[:, 1, :], ALU.subtract, [b_ZL], [b_zcar])
            TT("dve", zcar[:, 1, :], ZL[:, 2, :], ZL[:, 3, :], ALU.add, [b_ZL], [b_zcar])
            if st == 1 and c == nchunk - 1:
                FS, r_ = AR.f32([128, 2, 16]); b_FS = nbuf(r_)
                TT("dve", ZL[:, 0, :], zlr, sm(S_CL1), ALU.mult, [b_W[0], b_SM, b_zcar], [b_ZL])
                TT("dve", ZL[:, 1, :], zli, sm(S_SL1), ALU.mult, [b_W[1], b_SM], [b_ZL])
                TT("dve", ZL[:, 2, :], zlr, sm(S_SL1), ALU.mult, [b_W[0], b_SM], [b_ZL])
                TT("dve", ZL[:, 3, :], zli, sm(S_CL1), ALU.mult, [b_W[1], b_SM], [b_ZL])
                TT("dve", FS[:, 0, :], ZL[:, 0, :], ZL[:, 1, :], ALU.subtract, [b_ZL], [b_FS])
                TT("dve", FS[:, 1, :], ZL[:, 2, :], ZL[:, 3, :], ALU.add, [b_ZL], [b_FS])
                store_rows_T(lambda k: FS[:, 0, :], lambda k: [b_FS], 16, 128, D["sarp"][:, :])
                store_rows_T(lambda k: FS[:, 1, :], lambda k: [b_FS], 16, 128, D["saip"][:, :])

        def stage_D(c):
            WZ, b_W = WZs[c % 2], b_Ws[c % 2]
            c0 = c * LCH
            ti = c // 4
            yb, ybb = banks[4 + c % 2][:, :], bank_buf[4 + c % 2]
            for grp in range(4):
                (pp, ppb) = PP[grp]
                cs = COS[:, grp * 4:(grp + 1) * 4, :].rearrange("p a b -> p (a b)")
                sn = SIN[:, grp * 4:(grp + 1) * 4, :].rearrange("p a b -> p (a b)")
                zre = WZ[:, 0, grp * 4:(grp + 1) * 4, :].rearrange("p a b -> p (a b)")
                zim = WZ[:, 1, grp * 4:(grp + 1) * 4, :].rearrange("p a b -> p (a b)")
                pk = PPK[grp]
                srcs = ((zre, cs, 0, b_COS), (zim, sn, 1, b_SIN), (zre, sn, 0, b_SIN), (zim, cs, 1, b_COS))
                for kk, (zz, tb, wi, tbb) in enumerate(srcs):
                    eng_ = "pool" if (grp, kk) in DEMOD_POOL else "dve"
                    TT(eng_, pp[:, kk, :], zz, tb, ALU.mult, [b_W[wi], tbb], [pk[kk]])
                ct = grp
                yo = yb[:, ct * 128:(ct + 1) * 128]
                MM(yo, DD[:, ct, :], UAB[:, ct, c0:c0 + LCH], True, False, [b_DD, uab[ct][ti]], [ybb])
                for j in range(4):
                    p = grp * 4 + j
                    sl = slice(j * 128, (j + 1) * 128)
                    MM(yo, CR[:, p, :], pp[:, 0, sl], False, False, [b_CR, pk[0]], [ybb])
                    MM(yo, NCR[:, p, :], pp[:, 1, sl], False, False, [b_NCR, pk[1]], [ybb])
                    MM(yo, NCI[:, p, :], pp[:, 2, sl], False, False, [b_NCI, pk[2]], [ybb])
                    MM(yo, NCI[:, p, :], pp[:, 3, sl], False, j == 3, [b_NCI, pk[3]], [ybb])
            glu_front(yb, ybb, LCH, GAs[c % 2], zbank=6 + c % 2)

        emit_bu(0)
        emit_bu(1)
        stage_M(0)
        for c in range(nchunk):
            if c + 1 < nchunk:
                stage_M(c + 1)
            stage_D(c)
            if c >= 1:
                glu_back((c - 1) * LCH, LCH, (c - 1) // 4, GAs[(c - 1) % 2])
        glu_back((nchunk - 1) * LCH, LCH, (nchunk - 1) // 4, GAs[(nchunk - 1) % 2])
        b_W = b_Ws[0] + b_Ws[1]
        if st == 1:
            xoff[0] = X32_WORD0
            S0, b_S0 = xalloc(512, [128, 2, 16, NS], F32)
            SN, b_SN = xalloc(512, [128, 2, 16, NS], F32)
            SNB, b_SNB = xalloc(256, [128, 2, 16, NS], BF16)
            TQ, b_TQ = xalloc(512, [128, 2, 16, NS], F32)
            for ri, nm in enumerate(("sare", "saim")):
                for hf in range(2):
                    load_rows_T(D[nm][:, hf * 1024:(hf + 1) * 1024], NS, 1024,
                                lambda k0, nk, ri=ri, hf=hf: [("act", S0[:, ri, hf * 8 + k0:hf * 8 + k0 + nk, :], [b_S0])],
                                None)
            c0 = 1024
            bre, breb = bank()
            bim, bimb = bank()
            for p in range(16):
                MM(bre[:, p * NS:(p + 1) * NS], BTR[:, p, :], UAB[:, p // 4, c0:c0 + NS], True, True,
                   [b_BTR, uab[p // 4][2]], [breb])
                MM(bim[:, p * NS:(p + 1) * NS], BTI[:, p, :], UAB[:, p // 4, c0:c0 + NS], True, True,
                   [b_BTI, uab[p // 4][2]], [bimb])
            lbr = sm(S_LBR).unsqueeze(2).to_broadcast([128, 16, NS])
            lbi = sm(S_LBI).unsqueeze(2).to_broadcast([128, 16, NS])
            bre_v = bre[:, 0:16 * NS].rearrange("p (a b) -> p a b", a=16)
            bim_v = bim[:, 0:16 * NS].rearrange("p (a b) -> p a b", a=16)
            TT("dve", TQ[:, 0], S0[:, 0], lbr, ALU.mult, [b_S0, b_SM], [b_TQ])
            TT("dve", TQ[:, 1], S0[:, 1], lbi, ALU.mult, [b_S0, b_SM], [b_TQ])
            TT("dve", TQ[:, 0], TQ[:, 0], TQ[:, 1], ALU.subtract, [b_TQ], [b_TQ])
            TT("dve", SN[:, 0], TQ[:, 0], bre_v, ALU.add, [b_TQ, breb], [b_SN])
            TT("dve", TQ[:, 0], S0[:, 1], lbr, ALU.mult, [b_S0, b_SM], [b_TQ])
            TT("dve", TQ[:, 1], S0[:, 0], lbi, ALU.mult, [b_S0, b_SM], [b_TQ])
            TT("dve", TQ[:, 0], TQ[:, 0], TQ[:, 1], ALU.add, [b_TQ], [b_TQ])
            TT("dve", SN[:, 1], TQ[:, 0], bim_v, ALU.add, [b_TQ, bimb], [b_SN])
            CP("act", SNB, SN, [b_SN], [b_SNB])
            for ri, nm in enumerate(("sars", "sais")):
                for hf in range(2):
                    store_rows_T(lambda k, ri=ri, hf=hf: SN[:, ri, hf * 8 + k, :], lambda k: [b_SN], NS, 1024,
                                 D[nm][:, hf * 1024:(hf + 1) * 1024])
            yb, ybb = bank()
            for ct in range(4):
                yo = yb[:, ct * NS:(ct + 1) * NS]
                MM(yo, DD[:, ct, :], UAB[:, ct, c0:c0 + NS], True, False, [b_DD, uab[ct][2]], [ybb])
                for j in range(4):
                    p = ct * 4 + j
                    MM(yo, CR[:, p, :], SNB[:, 0, p, :], False, False, [b_CR, b_SNB], [ybb])
                    MM(yo, NCI[:, p, :], SNB[:, 1, p, :], False, j == 3, [b_NCI, b_SNB], [ybb])
            glu_front(yb, ybb, NS, GAs[0])
            glu_back(c0, NS, 2, GAs[0])
        for k in range(8):
            for t in range(3):
                inherit(xbb[k][t], b_W)
        for k in range(8):
            for t in range(3):
                inherit(x32b[k][t], scratch_bufs)
        load_inputs(st, "32")
        out_proj(st, D["w_out_ab"], lambda k, c0, n: YM[:, k, c0:c0 + n], lambda k, ti: ymb[k][ti], 0)
        AR.release(m0)

    for st in range(2):
        if "noin" not in stages and (st == 0 or "f1" not in stages):
            load_inputs(st, "b")
            if "me" not in stages:
                load_inputs(st, "32")
        if "me" in stages:
            mixer_even(st, pre_s5_hook=(lambda: store_outputs(0)) if (st == 1 and "noout" not in stages) else None)
        if "f0" in stages:
            ffn(st, 0)
        if "mo" in stages:
            mixer_odd(st)
        if "f1" in stages:
            ffn(st, 1, mid_hook=(lambda: load_inputs(1, "b")) if st == 0 else None)
        if "noout" not in stages and (st == 1 or "me" not in stages):
            store_outputs(st)
    kb.emit()
    return nc, kb


_CACHE = {}


def kernel(**inp):
    inp = {k: np.asarray(v) for k, v in inp.items()}
    if "nc" not in _CACHE:
        _CACHE["nc"], _CACHE["kb"] = build_program()
    nc = _CACHE["nc"]
    shared = host_params(inp)
    in_maps = []
    for i in range(NCORES):
        s0, s1 = NS * i, NS * (i + 1)
        m = dict(shared)
        m["xp"] = np.ascontiguousarray(inp["x_prompt"][i])
        m["xs"] = np.ascontiguousarray(inp["x_sample"][s0:s1, 0, :])
        m["sare"] = np.ascontiguousarray(inp["state_a_re"][0, s0:s1].reshape(NS, 2048))
        m["saim"] = np.ascontiguousarray(inp["state_a_im"][0, s0:s1].reshape(NS, 2048))
        m["ccc"] = np.ascontiguousarray(inp["cache_c_conv"][0, s0:s1].reshape(NS * 30, DM))
        m["cfc"] = np.ascontiguousarray(inp["cache_ffn_conv"][:, s0:s1].reshape(2, NS * 2, DFF))
        in_maps.append({k: np.ascontiguousarray(v, dtype=np.float32) for k, v in m.items()})
    res = run_bass_kernel_spmd(nc, in_maps, core_ids=list(range(NCORES)))
    R = res.results
    f32 = np.float32
    yp = np.stack([R[i]["yp"] for i in range(NCORES)]).astype(f32)
    ys = np.concatenate([R[i]["ys"] for i in range(NCORES)])[:, None, :].astype(f32)
    sarp = np.stack([R[i]["sarp"].reshape(32, 64) for i in range(NCORES)])[None].astype(f32)
    saip = np.stack([R[i]["saip"].reshape(32, 64) for i in range(NCORES)])[None].astype(f32)
    sars = np.concatenate([R[i]["sars"].reshape(NS, 32, 64) for i in range(NCORES)])[None].astype(f32)
    sais = np.concatenate([R[i]["sais"].reshape(NS, 32, 64) for i in range(NCORES)])[None].astype(f32)
    sbv = np.concatenate([R[i]["sbv"] for i in range(NCORES)])[None, :, None, :].astype(f32)
    ccp = np.stack([R[i]["ccp"] for i in range(NCORES)])[None].astype(f32)
    ccs = np.concatenate([R[i]["ccs"].reshape(NS, 30, DM) for i in range(NCORES)])[None].astype(f32)
    cfp = np.stack([R[i]["cfp"] for i in range(NCORES)], axis=1).astype(f32)
    cfs = np.concatenate([R[i]["cfs"].reshape(2, NS, 2, DFF) for i in range(NCORES)], axis=1).astype(f32)
    return (yp, ys, sarp, saip, sars, sais, sbv, ccp, ccs, cfp, cfs)
```

```python
import math
import numpy as np
from contextlib import ExitStack
import concourse.bass as bass
import concourse.mybir as mybir
from concourse.bass_utils import run_bass_kernel_spmd

F32 = mybir.dt.float32
BF16 = mybir.dt.bfloat16
I32 = mybir.dt.int32
ALU = mybir.AluOpType
AF = mybir.ActivationFunctionType
AX = mybir.AxisListType

ENGS = ("pe", "act", "dve", "pool", "sp")
ALPHA = (2.0 * 2) ** 0.25
LN_EPS = 1e-5
NCORES = 8
T = 2048
NS = 16
DM = 1024
DFF = 2816
NH = 22
LCH = 128
ARENA_WORDS = 53100


class Buf:
    __slots__ = ("name", "w", "r", "excl")

    def __init__(self, name="", excl=False):
        self.name = name
        self.w = None
        self.r = {}
        self.excl = excl


def _tok_newer(a, b):
    if a[0] == "e":
        return a[2] > b[2]
    return a[2] > b[2]


def inherit(newb, olds):
    for ob in olds:
        toks = list(ob.r.items())
        if ob.w is not None:
            t = ob.w
            key = t[1] if t[0] == "e" else ("d", t[1])
            toks.append((key, t))
        for key, t in toks:
            cur = newb.r.get(key)
            if cur is None or _tok_newer(t, cur):
                newb.r[key] = t


class KB:
    def __init__(self, nc, n_dma_sems=12):
        self.nc = nc
        self.stack = ExitStack()
        self.prog = []
        self.n_eng = {e: 0 for e in ENGS}
        self.n_dma_sems = n_dma_sems
        self.dma_slot = {e: 0 for e in ENGS}
        self.dma_last = {}
        self.dma_val = {}

    def sb(self, name, shape, dtype):
        return self.stack.enter_context(self.nc.sbuf_tensor(name, list(shape), dtype))

    def ps(self, name, shape, dtype=F32):
        return self.stack.enter_context(self.nc.psum_tensor(name, list(shape), dtype))

    def _deps(self, eng, reads, writes):
        deps = []
        for b in reads:
            if b.w is not None:
                deps.append(b.w)
            if b.excl:
                for key, tok in b.r.items():
                    if key != eng:
                        deps.append(tok)
        for b in writes:
            if b.w is not None:
                deps.append(b.w)
            deps.extend(b.r.values())
        return deps

    def op(self, eng, fn, reads=(), writes=()):
        deps = self._deps(eng, reads, writes)
        idx = self.n_eng[eng]
        self.n_eng[eng] += 1
        tok = ("e", eng, idx)
        self.prog.append((eng, fn, deps, tok, False))
        for b in reads:
            b.r[eng] = tok
        for b in writes:
            b.w = tok
            b.r = {}
        return tok

    def dma(self, eng, out, in_, reads=(), writes=(), **kw):
        deps = self._deps(eng, reads, writes)
        slot = self.dma_slot[eng]
        self.dma_slot[eng] = (slot + 1) % self.n_dma_sems
        key = (eng, slot)
        if key in self.dma_last:
            deps.append(self.dma_last[key])
        val = self.dma_val.get(key, 0) + 16
        self.dma_val[key] = val
        tok = ("d", key, val)
        self.dma_last[key] = tok
        self.n_eng[eng] += 1

        def fn(e, out=out, in_=in_, kw=kw):
            return e.dma_start(out=out, in_=in_, **kw)
        self.prog.append((eng, fn, deps, tok, True))
        for b in reads:
            b.r[("d", key)] = tok
        for b in writes:
            b.w = tok
            b.r = {}
        return tok

    def emit(self, final_eng="sp"):
        nc = self.nc
        fin_deps = list(self.dma_last.values())
        self.prog.append((final_eng, None, fin_deps, ("e", final_eng, self.n_eng[final_eng]), False))
        marked = set()
        for eng, fn, deps, tok, is_dma in self.prog:
            for d in deps:
                if d[0] == "e":
                    if d[1] == "pe" and eng == "pe":
                        continue
                    marked.add(d)
        cnt = {e: 0 for e in ENGS}
        count_at = {}
        for eng, fn, deps, tok, is_dma in self.prog:
            if tok[0] == "e" and tok in marked:
                cnt[eng] += 1
                count_at[tok] = cnt[eng]
        esem = {e: self.stack.enter_context(nc.semaphore("s_" + e)) for e in ENGS}
        dsem = {}
        for key in self.dma_val:
            dsem[key] = self.stack.enter_context(nc.semaphore("d_%s%d" % key))
        streams = {e: [] for e in ENGS}
        clock = {e: {} for e in ENGS}
        tok_clock = {}
        n_wait = 0
        for eng, fn, deps, tok, is_dma in self.prog:
            ck = clock[eng]
            for d in deps:
                if d[0] == "e":
                    if d[1] == "pe" and eng == "pe":
                        continue
                    key = ("e", d[1])
                    val = count_at[d]
                    sem = esem[d[1]]
                else:
                    key = ("d", d[1])
                    val = d[2]
                    sem = dsem[d[1]]
                if ck.get(key, 0) >= val:
                    continue
                streams[eng].append(("w", sem, val))
                n_wait += 1
                for k2, v2 in tok_clock[d].items():
                    if ck.get(k2, 0) < v2:
                        ck[k2] = v2
            if fn is None:
                continue
            if is_dma:
                streams[eng].append(("d", fn, dsem[tok[1]]))
                c2 = dict(ck)
                c2[("d", tok[1])] = tok[2]
                tok_clock[tok] = c2
            else:
                inc = tok in marked
                streams[eng].append(("o", fn, esem[eng] if inc else None))
                if inc:
                    c2 = dict(ck)
                    c2[("e", eng)] = count_at[tok]
                    tok_clock[tok] = c2
        self.stats = dict(n_wait=n_wait, n_inst={e: len(streams[e]) for e in ENGS}, marked=len(marked), cnt=cnt)

        def run(e, items):
            for it in items:
                if it[0] == "w":
                    e.wait_ge(it[1], it[2])
                elif it[0] == "d":
                    it[1](e).then_inc(it[2], 16)
                else:
                    ins = it[1](e)
                    if it[2] is not None:
                        ins.then_inc(it[2], 1)

        with nc.Block() as block:
            @block.tensor
            def _(e):
                run(e, streams["pe"])

            @block.scalar
            def _(e):
                run(e, streams["act"])

            @block.vector
            def _(e):
                run(e, streams["dve"])

            @block.gpsimd
            def _(e):
                run(e, streams["pool"])

            @block.sync
            def _(e):
                run(e, streams["sp"])
        self.stack.close()


class Arena:
    def __init__(self, t, nwords):
        self.t = t
        self.n = nwords
        self.off = 0
        self.recs = []

    def mark(self):
        return self.off

    def release(self, m):
        self.off = m

    def _alloc(self, words, drop=True):
        words = (words + 7) // 8 * 8
        s = self.off
        assert s + words <= self.n, ("arena overflow", s, words, self.n)
        self.off = s + words
        rec = [s, s + words, [], None, False]
        olds = []
        keep = []
        for r in self.recs:
            if r[0] < rec[1] and rec[0] < r[1]:
                olds.extend(r[2])
                if drop and r[0] >= rec[0] and r[1] <= rec[1] and not r[4]:
                    continue
            keep.append(r)
        self.recs = keep
        self.recs.append(rec)
        rec[3] = olds
        return rec

    def f32(self, shape):
        n = int(np.prod(shape[1:]))
        rec = self._alloc(n)
        ap = self.t[:shape[0], rec[0]:rec[0] + n]
        return self._shape(ap, shape), rec

    def bf16(self, shape):
        n = int(np.prod(shape[1:]))
        rec = self._alloc((n + 1) // 2)
        ap = self.t[:shape[0], rec[0]:rec[0] + (n + 1) // 2].bitcast(BF16)[:, 0:n]
        return self._shape(ap, shape), rec

    def at(self, word_off, words):
        save = self.off
        self.off = word_off
        rec = self._alloc(words, drop=False)
        self.off = save
        return rec

    @staticmethod
    def _shape(ap, shape):
        if len(shape) == 2:
            return ap
        if len(shape) == 3:
            return ap.rearrange("p (a b) -> p a b", a=shape[1])
        if len(shape) == 4:
            return ap.rearrange("p (a b c) -> p a b c", a=shape[1], b=shape[2])
        raise ValueError(shape)


def nbuf(rec, name=""):
    b = Buf(name)
    inherit(b, rec[3])
    rec[2].append(b)
    return b


PC = {}


def _pcol_layout():
    off = 0
    for l in range(2):
        for nm in ("mixg", "mixb", "ffng", "ffnb"):
            PC[(nm, l)] = off
            off += 8
    for nm in ("lncg", "lncb", "convcb"):
        PC[nm] = off
        off += 8
    PC["convcw"] = off
    off += 31 * 8
    for l in range(2):
        PC[("fcw", l)] = off
        off += 3 * NH
        PC[("fcb", l)] = off
        off += NH
    PC["s5d"] = off
    off += 4
    PC["glub"] = off
    off += 4
    PC["n"] = off


_pcol_layout()


def _cols(v, ntile):
    return np.ascontiguousarray(v.reshape(ntile, 128).T)


def host_params(inp):
    pc = np.zeros((128, PC["n"]), np.float32)
    for l in range(2):
        pc[:, PC[("mixg", l)]:PC[("mixg", l)] + 8] = _cols(inp["ln_mix_g"][l], 8)
        pc[:, PC[("mixb", l)]:PC[("mixb", l)] + 8] = _cols(inp["ln_mix_b"][l], 8)
        pc[:, PC[("ffng", l)]:PC[("ffng", l)] + 8] = _cols(inp["ln_ffn_g"][l], 8)
        pc[:, PC[("ffnb", l)]:PC[("ffnb", l)] + 8] = _cols(inp["ln_ffn_b"][l], 8)
        for k in range(3):
            pc[:, PC[("fcw", l)] + k * NH:PC[("fcw", l)] + (k + 1) * NH] = _cols(inp["ffn_conv_w"][l, k], NH)
        pc[:, PC[("fcb", l)]:PC[("fcb", l)] + NH] = _cols(inp["ffn_conv_b"][l], NH)
    pc[:, PC["lncg"]:PC["lncg"] + 8] = _cols(inp["ln_c_g"][0], 8)
    pc[:, PC["lncb"]:PC["lncb"] + 8] = _cols(inp["ln_c_b"][0], 8)
    pc[:, PC["convcb"]:PC["convcb"] + 8] = _cols(inp["conv_c_b"][0], 8)
    for k in range(31):
        pc[:, PC["convcw"] + k * 8:PC["convcw"] + (k + 1) * 8] = _cols(inp["conv_c_w"][0, k], 8)
    pc[:, PC["s5d"]:PC["s5d"] + 4] = _cols(inp["s5_d"][0], 4)
    pc[:, PC["glub"]:PC["glub"] + 4] = _cols(inp["s5_glu_b"][0], 4)

    def pairl(a):
        return np.ascontiguousarray(a.reshape(16, 2, 64).transpose(1, 2, 0).reshape(128, 16))
    out = {"pcol": pc}
    out["lamre"] = pairl(inp["s5_lam_re"][0])
    out["lamim"] = pairl(inp["s5_lam_im"][0])
    out["logdt"] = pairl(np.broadcast_to(inp["s5_log_dt"][0][:, None], (32, 64)))

    def bx(b):
        o = np.zeros((128, 16, 128), np.float32)
        for p in range(16):
            for gi in range(2):
                r0 = 32 * (p % 4) + 16 * gi
                o[64 * gi:64 * gi + 64, p, r0:r0 + 16] = b[2 * p + gi]
        return o

    def cx(c):
        o = np.zeros((128, 16, 128), np.float32)
        for p in range(16):
            for gi in range(2):
                r0 = 32 * (p % 4) + 16 * gi
                o[64 * gi:64 * gi + 64, p, r0:r0 + 16] = c[2 * p + gi].T
        return o
    cwt = np.zeros((32, DM), np.float32)
    cwt[:31] = inp["conv_c_w"][0]
    out["convst"] = np.ascontiguousarray(
        cwt.reshape(8, 4, 8, 4, 32).transpose(1, 4, 2, 3, 0).reshape(128, 256))
    out["brex"] = bx(inp["s5_b_re"][0])
    out["bimx"] = bx(inp["s5_b_im"][0])
    out["crex"] = cx(inp["s5_c_re"][0])
    out["cimx"] = cx(inp["s5_c_im"][0])
    w = inp["sgu_w"][0]
    out["sguwT"] = np.ascontiguousarray(w.transpose(2, 0, 1))
    out["sgubc"] = np.ascontiguousarray(np.broadcast_to(inp["sgu_b"][0][None], (128, 4, 128)))
    out["sgw00"] = np.ascontiguousarray(np.broadcast_to(np.repeat(w[:, 0, 0], 128)[None], (NS, 512)))
    out["sgb0"] = np.ascontiguousarray(np.broadcast_to(np.repeat(inp["sgu_b"][0][:, 0], 128)[None], (NS, 512)))
    out["sglng"] = np.ascontiguousarray(np.broadcast_to(inp["sgu_ln_g"][0][None], (128, 512)))
    out["sglnb"] = np.ascontiguousarray(np.broadcast_to(inp["sgu_ln_b"][0][None], (128, 512)))
    for k in ("w_in_ab", "s5_glu_w", "w_out_ab", "w_in_c", "w_out_c"):
        out[k] = np.ascontiguousarray(inp[k][0])
    for k in ("ffn_w_gate", "ffn_w_up", "ffn_w_down"):
        out[k] = np.ascontiguousarray(inp[k])
    return out


IN_SHAPES = {
    "xp": [T, DM], "xs": [NS, DM], "sare": [NS, 2048], "saim": [NS, 2048], "ccc": [NS * 30, DM],
    "cfc": [2, NS * 2, DFF], "pcol": [128, PC["n"]], "lamre": [128, 16], "lamim": [128, 16], "logdt": [128, 16],
    "brex": [128, 16, 128], "bimx": [128, 16, 128], "crex": [128, 16, 128], "cimx": [128, 16, 128],
    "sguwT": [128, 4, 128], "sgubc": [128, 4, 128], "sgw00": [NS, 512], "sgb0": [NS, 512],
    "sglng": [128, 512], "sglnb": [128, 512], "convst": [128, 256],
    "w_in_ab": [1024, 1536], "s5_glu_w": [512, 512], "w_out_ab": [1024, 1024], "w_in_c": [1024, 2048],
    "w_out_c": [1024, 1024], "ffn_w_gate": [2, 1024, DFF], "ffn_w_up": [2, 1024, DFF], "ffn_w_down": [2, DFF, 1024],
}
OUT_SHAPES = {
    "yp": [T, DM], "ys": [NS, DM], "sarp": [16, 128], "saip": [16, 128], "sars": [NS, 2048], "sais": [NS, 2048],
    "sbv": [NS, 512], "ccp": [30, DM], "ccs": [NS * 30, DM], "cfp": [2, 2, DFF], "cfs": [2, NS * 2, DFF],
}


def build_program(stages=("me", "f0", "mo", "f1")):
    nc = bass.Bass("TRN2", target_bir_lowering=False)
    kb = KB(nc)
    D = {}
    compute = any(x in stages for x in ("me", "f0", "mo", "f1"))
    for k, s in IN_SHAPES.items():
        if not compute and (k.startswith("w_") or k.startswith("ffn_w") or k == "s5_glu_w"):
            continue
        D[k] = nc.dram_tensor(k, list(s), F32, kind="ExternalInput").ap()
    for k, s in OUT_SHAPES.items():
        D[k] = nc.dram_tensor(k, list(s), F32, kind="ExternalOutput").ap()

    GSD = [(nc.dram_tensor("gsd%d" % i_, [128, 1056], BF16).ap(), Buf("gsd%d" % i_)) for i_ in range(3)]
    S5C = nc.dram_tensor("s5c_scratch", [128, 8192], F32).ap()
    b_S5C = Buf("s5c")
    S5CB = nc.dram_tensor("s5cb_scratch", [128, 16384], BF16).ap()
    b_S5CB = Buf("s5cb")
    AR_t = kb.sb("arena", [128, ARENA_WORDS], F32)
    AR = Arena(AR_t, ARENA_WORDS)
    banks = [kb.ps("bank%d" % i, [128, 512], F32) for i in range(8)]
    bank_buf = [Buf("bank%d" % i, excl=True) for i in range(8)]
    bank_rr = [0]

    def bank():
        i = bank_rr[0]
        bank_rr[0] = (i + 1) % 8
        return banks[i][:, :], bank_buf[i]

    def MM(out, lhsT, rhs, start, stop, reads, writes):
        kb.op("pe", lambda e: e.matmul(out, lhsT, rhs, start=start, stop=stop), reads, writes)

    def TR(out, in_, ident_ap, reads, writes):
        kb.op("pe", lambda e: e.transpose(out, in_, ident_ap), reads, writes)

    def TT(eng, out, a, b, op, reads, writes):
        kb.op(eng, lambda e: e.tensor_tensor(out, a, b, op), reads, writes)

    def STT(out, a, s, b, op0, op1, reads, writes):
        kb.op("dve", lambda e: e.scalar_tensor_tensor(out, a, s, b, op0, op1), reads, writes)

    def TS(eng, out, a, s1, s2, op0, op1, reads, writes):
        if s2 is None:
            kb.op(eng, lambda e: e.tensor_scalar(out, a, s1, None, op0), reads, writes)
        else:
            kb.op(eng, lambda e: e.tensor_scalar(out, a, s1, s2, op0, op1), reads, writes)

    def ACT(out, in_, func, reads, writes, bias=None, scale=None):
        kw = {}
        if bias is not None:
            kw["bias"] = bias
        if scale is not None:
            kw["scale"] = scale
        kb.op("act", lambda e: e.activation(out, in_, func, **kw), reads, writes)

    def CP(eng, out, in_, reads, writes):
        if eng == "act":
            kb.op("act", lambda e: e.copy(out, in_), reads, writes)
        else:
            kb.op(eng, lambda e: e.tensor_copy(out, in_), reads, writes)

    def MEMSET(eng, out, val, writes):
        kb.op(eng, lambda e: e.memset(out, val), (), writes)

    NCOL = 1024 + NS
    X32, rX32 = AR.f32([128, 8, NCOL])
    XB, rXB = AR.bf16([128, 8, NCOL])
    XB_WORD0 = rXB[0]
    XB_WORDS = rXB[1] - rXB[0]
    rX32[4] = True
    rXB[4] = True
    slots = []
    for i in range(3):
        ap, rec = AR.bf16([128, 4096])
        slots.append((ap, nbuf(rec, "slot%d" % i)))
    slot_rr = [0]
    ident, r_ = AR.f32([128, 128]); b_ident = nbuf(r_)
    identb, r_ = AR.bf16([128, 128]); b_identb = nbuf(r_)
    ones_c, r_ = AR.bf16([128, 128]); b_ones_c = nbuf(r_)
    pcol, r_ = AR.f32([128, PC["n"]]); b_pcol = nbuf(r_)
    cst, r_ = AR.f32([128, 8]); b_cst = nbuf(r_)
    m32, r_ = AR.bf16([128, 32]); b_m32 = nbuf(r_)
    m32f, r_ = AR.f32([128, 32]); b_m32f = nbuf(r_)
    convst, r_ = AR.f32([128, 256]); b_convst = nbuf(r_)
    stg = []
    for i in range(2):
        ap, rec = AR.f32([128, 1024])
        stg.append((ap, nbuf(rec, "stg%d" % i)))
    stg_rr = [0]
    ln_rb4, r_ = AR.bf16([128, 4, 512]); b_ln_rb4 = nbuf(r_)
    ln_sq4, r_ = AR.bf16([128, 4, 512]); b_ln_sq4 = nbuf(r_)
    ln_t = [AR.f32([128, 512]) for _ in range(2)]
    ln_t2 = [AR.f32([128, 512]) for _ in range(2)]
    ln_st = [AR.f32([128, 512]) for _ in range(2)]
    ln_t = [(a, nbuf(r)) for a, r in ln_t]
    ln_t2 = [(a, nbuf(r)) for a, r in ln_t2]
    ln_st = [(a, nbuf(r)) for a, r in ln_st]
    ln_rr = [0]
    halo_ffn, r_ = AR.f32([128, 2, NH, 2]); b_halo_ffn = [[nbuf(r_) for _ in range(NH)] for _ in range(2)]
    halo_g, r_ = AR.bf16([128, 8, 30]); b_halo_g = nbuf(r_)
    zcar, r_ = AR.f32([128, 2, 16]); b_zcar = nbuf(r_)
    STAGE0 = AR.mark()

    x32b = [[Buf("x32_%d_%d" % (k, t)) for t in range(3)] for k in range(8)]
    xbb = [[Buf("xb_%d_%d" % (k, t)) for t in range(3)] for k in range(8)]
    for k in range(8):
        for t in range(3):
            rX32[2].append(x32b[k][t])
            rXB[2].append(xbb[k][t])

    def colP(off):
        return pcol[:, off:off + 1]

    MEMSET("pool", ident, 0.0, [b_ident])
    kb.op("pool", lambda e: e.affine_select(ident, ident, pattern=[[-1, 128]], compare_op=ALU.not_equal, fill=1.0,
                                            base=0, channel_multiplier=1), [b_ident], [b_ident])
    CP("dve", identb, ident, [b_ident], [b_identb])
    MEMSET("dve", ones_c, 1.0 / 1024.0, [b_ones_c])
    MEMSET("dve", cst[:, 0:1], -math.pi, [b_cst])
    kb.dma("sp", pcol, D["pcol"], writes=[b_pcol])
    kb.dma("sp", convst, D["convst"], writes=[b_convst])
    TT("dve", m32f, ident[:, 0:32], ident[:, 32:64], ALU.add, [b_ident], [b_m32f])
    TT("dve", m32f, m32f, ident[:, 64:96], ALU.add, [b_ident, b_m32f], [b_m32f])
    TT("dve", m32f, m32f, ident[:, 96:128], ALU.add, [b_ident, b_m32f], [b_m32f])
    CP("dve", m32, m32f, [b_m32f], [b_m32])
    MEMSET("dve", halo_g, 0.0, [b_halo_g])
    for l in range(2):
        for j in range(NH):
            MEMSET("pool", halo_ffn[:, l, j, :], 0.0, [b_halo_ffn[l][j]])

    def load_w2(dram_ap, kt, ncols):
        i = slot_rr[0]
        slot_rr[0] = (i + 1) % 3
        ap, b = slots[i]
        view = ap[:, 0:kt * ncols].rearrange("p (k n) -> p k n", k=kt)
        src = dram_ap.rearrange("(k p) n -> p k n", p=128)
        chunks = []
        k0 = 0
        while k0 < kt:
            k1 = min(kt, k0 + 8)
            cb = Buf("wchunk")
            inherit(cb, [b] + slot_chunks[i])
            kb.dma("pool", view[:, k0:k1, :], src[:, k0:k1, :], writes=[cb])
            chunks.append(cb)
            k0 = k1
        slot_chunks[i] = chunks
        return view, chunks

    slot_chunks = [[], [], []]

    def load_wm(blocks, kt):
        i = slot_rr[0]
        slot_rr[0] = (i + 1) % 3
        ap, b = slots[i]
        tot = sum(nc_ for _, nc_ in blocks)
        view = ap[:, 0:kt * tot].rearrange("p (k n) -> p k n", k=kt)
        bufs = []
        off = 0
        olds = [b] + slot_chunks[i]
        for dram_ap, ncols in blocks:
            src = dram_ap.rearrange("(k p) n -> p k n", p=128)
            cb = Buf("wchunk")
            inherit(cb, olds)
            kb.dma("pool", view[:, :, off:off + ncols], src, writes=[cb])
            bufs.append(cb)
            off += ncols
        slot_chunks[i] = bufs
        return view, bufs

    def wb_of(chunks, k):
        return chunks[k // 8]

    def tiles_of(st):
        t = [(0, 512, 0), (512, 512, 1)]
        if st == 1:
            t.append((1024, NS, 2))
        return t

    def next_stg():
        i = stg_rr[0]
        stg_rr[0] = (i + 1) % 2
        return stg[i]

    def load_rows_T(dram_rows, R, C, dst_fn, dst_bufs_fn, stage=None):
        if stage is None:
            sap, sb_ = next_stg()
        else:
            sap, sb_ = stage
        kb.dma("sp", sap[:R, 0:C], dram_rows, writes=[sb_])
        if "in1" in stages:
            return
        nkb = C // 128
        per = max(1, min(512 // R, nkb))
        k0 = 0
        while k0 < nkb:
            nk = min(per, nkb - k0)
            bk, bb = bank()
            for j in range(nk):
                TR(bk[:, j * R:(j + 1) * R], sap[:R, (k0 + j) * 128:(k0 + j + 1) * 128], ident[:R, :R],
                   [sb_, b_ident], [bb])
            dsts = dst_fn(k0, nk)
            if "in2" in stages:
                dsts = []
            if "in3" in stages:
                dsts = dsts[:1]
            for (eng, dap, dbufs) in dsts:
                src = bk[:, 0:nk * R].rearrange("p (a b) -> p a b", a=nk)
                CP(eng, dap, src, [bb], dbufs)
            k0 += nk

    def store_rows_T(src_fn, src_bufs_fn, R, C, dram_rows, stage=None):
        if stage is None:
            sap, sb_ = next_stg()
        else:
            sap, sb_ = stage
        nkb = C // 128
        k0 = 0
        while k0 < nkb:
            nk = min(4, nkb - k0)
            bk, bb = bank()
            for j in range(nk):
                TR(bk[:R, j * 128:(j + 1) * 128], src_fn(k0 + j), ident, src_bufs_fn(k0 + j) + [b_ident], [bb])
            CP("act", sap[:R, k0 * 128:(k0 + nk) * 128], bk[:R, 0:nk * 128], [bb], [sb_])
            k0 += nk
        kb.dma("sp", dram_rows, sap[:R, 0:C], reads=[sb_])

    def ln_feat(n, nk, src, src_b, gcol, bcol, dst32, dst32_b, dstb, dstb_b, func=AF.Identity, ones=None,
                src4=None, dst32_4=None, dstb_4=None):
        ones = ones_c if ones is None else ones
        bm, bmb = bank()
        be, beb = bank()
        if src4 is not None:
            for k0 in range(0, nk, 4):
                sbs = [src_b(k) for k in range(k0, k0 + 4)]
                CP("act", ln_rb4[:, :, :n], src4(k0), sbs, [b_ln_rb4])
                for j in range(4):
                    k = k0 + j
                    MM(bm[:, :n], ones, ln_rb4[:, j, :n], k == 0, k == nk - 1, [b_ln_rb4, b_ones_c], [bmb])
                ACT(ln_sq4[:, :, :n], src4(k0), AF.Square, sbs, [b_ln_sq4])
                for j in range(4):
                    k = k0 + j
                    MM(be[:, :n], ones, ln_sq4[:, j, :n], k == 0, k == nk - 1, [b_ln_sq4, b_ones_c], [beb])
        else:
            raise AssertionError("grouped source view required")
        (m2, m2b), (rs, rsb) = ln_st
        ACT(m2[:, :n], bm[:, :n], AF.Square, [bmb], [m2b])
        TT("dve", m2[:, :n], be[:, :n], m2[:, :n], ALU.subtract, [beb, m2b], [m2b])
        TS("dve", m2[:, :n], m2[:, :n], 0.0, LN_EPS, ALU.max, ALU.add, [m2b], [m2b])
        ACT(m2[:, :n], m2[:, :n], AF.Ln, [m2b], [m2b])
        ACT(rs[:, :n], m2[:, :n], AF.Exp, [m2b], [rsb], scale=-0.5)
        for k in range(nk):
            i = ln_rr[0]
            ln_rr[0] = (i + 1) % 2
            (t1, t1b), (t2, t2b) = ln_t[i], ln_t2[i]
            TT("dve", t1[:, :n], src(k), bm[:, :n], ALU.subtract, [src_b(k), bmb], [t1b])
            TT("dve", t2[:, :n], t1[:, :n], rs[:, :n], ALU.mult, [t1b, rsb], [t2b])
            if dst32 is not None:
                ACT(dst32(k), t2[:, :n], AF.Identity, [t2b, b_pcol], [dst32_b(k)], bias=bcol(k), scale=gcol(k))
                if dstb is not None:
                    if dstb_4 is None:
                        CP("act", dstb(k), dst32(k), [dst32_b(k)], [dstb_b(k)])
                    elif k % 4 == 3:
                        CP("act", dstb_4(k - 3), dst32_4(k - 3), [dst32_b(kk) for kk in range(k - 3, k + 1)],
                           [dstb_b(kk) for kk in range(k - 3, k + 1)])
            else:
                ACT(dstb(k), t2[:, :n], func, [t2b, b_pcol], [dstb_b(k)], bias=bcol(k), scale=gcol(k))

    def residual_ln(st, l, which, tiles=None):
        g0 = PC[(which + "g", l)]
        b0 = PC[(which + "b", l)]
        last = (which == "ffn" and l == 1)
        for (c0, n, ti) in (tiles_of(st) if tiles is None else tiles):
            ln_feat(n, 8,
                    lambda k: X32[:, k, c0:c0 + n], lambda k: x32b[k][ti],
                    lambda k: colP(g0 + k), lambda k: colP(b0 + k),
                    lambda k: X32[:, k, c0:c0 + n], lambda k: x32b[k][ti],
                    None if last else (lambda k: XB[:, k, c0:c0 + n]), lambda k: xbb[k][ti],
                    src4=lambda k0: X32[:, k0:k0 + 4, c0:c0 + n], dst32_4=lambda k0: X32[:, k0:k0 + 4, c0:c0 + n],
                    dstb_4=lambda k0: XB[:, k0:k0 + 4, c0:c0 + n])

    def out_proj(st, wdram, ymf, ymbf, l, pre_ln=None):
        wblk = [load_w2(wdram[:, blk * 512:(blk + 1) * 512], 8, 512) for blk in range(2)]
        for (c0, n, ti) in tiles_of(st):
            for blk in range(2):
                wv, wch = wblk[blk]
                for m in range(4):
                    bk, bb = bank()
                    for k in range(8):
                        MM(bk[:, :n], wv[:, k, m * 128:(m + 1) * 128], ymf(k, c0, n), k == 0, k == 7,
                           [wb_of(wch, k), ymbf(k, ti)], [bb])
                    km = blk * 4 + m
                    STT(X32[:, km, c0:c0 + n], X32[:, km, c0:c0 + n], ALPHA, bk[:, :n], ALU.mult, ALU.add,
                        [bb, x32b[km][ti]], [x32b[km][ti]])
            if pre_ln is not None:
                pre_ln(ti)
            residual_ln(st, l, "mix", [(c0, n, ti)])

    def load_inputs(st, mode):
        def dsts_for(c, ti, w):
            def f(k0, nk):
                if mode == "b":
                    return [("act", XB[:, k0:k0 + nk, c:c + w], [xbb[k][ti] for k in range(k0, k0 + nk)])]
                return [("act", X32[:, k0:k0 + nk, c:c + w], [x32b[k][ti] for k in range(k0, k0 + nk)])]
            return f
        for rbk in range(8):
            r0 = st * 1024 + rbk * 128
            c = rbk * 128
            load_rows_T(D["xp"][r0:r0 + 128, :], 128, 1024, dsts_for(c, c // 512, 128), None)
        if st == 1:
            load_rows_T(D["xs"][:, :], NS, 1024, dsts_for(1024, 2, NS), None)

    def store_outputs(st):
        for rbk in range(8):
            r0 = st * 1024 + rbk * 128
            c = rbk * 128
            ti = c // 512
            store_rows_T(lambda k, c=c: X32[:, k, c:c + 128], lambda k, ti=ti: [x32b[k][ti]], 128, 1024,
                         D["yp"][r0:r0 + 128, :])
        if st == 1:
            store_rows_T(lambda k: X32[:, k, 1024:1024 + NS], lambda k: [x32b[k][2]], NS, 1024, D["ys"][:, :])

    def ffn(st, l, mid_hook=None):
        m0 = AR.mark()
        HID, rH = AR.bf16([128, NH, NCOL])
        hidb = [[nbuf(rH) for _ in range(3)] for _ in range(NH)]
        GRJ = [AR.f32([128, 2 + 1024]) for _ in range(6)]
        GRJ = [(ap_, [nbuf(r_), nbuf(r_), nbuf(r_)]) for ap_, r_ in GRJ]
        AA = [AR.f32([128, 512]) for _ in range(2)]
        AA = [(a_, nbuf(r_)) for a_, r_ in AA]
        SG = [AR.f32([128, 512]) for _ in range(2)]
        SG = [(a_, nbuf(r_)) for a_, r_ in SG]
        rr = [0]
        wg, wu, wd = D["ffn_w_gate"][l], D["ffn_w_up"][l], D["ffn_w_down"][l]
        fcw, fcb = PC[("fcw", l)], PC[("fcb", l)]
        if st == 1:
            CF, rCF = AR.f32([128, NH, NS * 2])
            b_CF = nbuf(rCF)
            NEWG, rNG = AR.f32([128, NH, NS])
            b_NEWG = nbuf(rNG)
            CFP, rCFP = AR.f32([128, NH, 2])
            b_CFP = nbuf(rCFP)
            bigst, rbs = AR.f32([128, DFF])
            b_bigst = nbuf(rbs)
            load_rows_T(D["cfc"][l], NS * 2, DFF,
                        lambda k0, nk: [("act", CF[:, k0:k0 + nk, :], [b_CF])], None, stage=(bigst, b_bigst))
        units = list(range(NH // 2))
        for g0 in range(0, len(units), 3):
            grp_units = units[g0:g0 + 3]
            loaded = {}
            for un in grp_units:
                hb0 = un * 2
                cs_ = slice(hb0 * 128, (hb0 + 2) * 128)
                loaded[un] = load_wm([(wg[:, cs_], 256), (wu[:, cs_], 256)], 8)
                for j in range(2):
                    hj = hb0 + j
                    grj, (gh_b, g0_b, g1_b) = GRJ[(un % 3) * 2 + j]
                    CP("act", grj[:, 0:2], halo_ffn[:, l, hj, :], [b_halo_ffn[l][hj]], [gh_b])
            for (c0, n, ti) in tiles_of(st):
                for un in grp_units:
                    hb0 = un * 2
                    sv, (gch, uch) = loaded[un]
                    for j in range(2):
                        hj = hb0 + j
                        grj, gbufs = GRJ[(un % 3) * 2 + j]
                        gb, gbb = bank()
                        for k in range(8):
                            MM(gb[:, :n], sv[:, k, j * 128:(j + 1) * 128], XB[:, k, c0:c0 + n], k == 0, k == 7,
                               [gch, xbb[k][ti]], [gbb])
                        ub, ubb = bank()
                        for k in range(8):
                            MM(ub[:, :n], sv[:, k, 256 + j * 128:256 + (j + 1) * 128], XB[:, k, c0:c0 + n], k == 0, k == 7,
                               [uch, xbb[k][ti]], [ubb])
                        i = rr[0]
                        rr[0] = (i + 1) % 2
                        (aa, aab), (sg, sgb) = AA[i], SG[i]
                        w0, w1, w2, bc = colP(fcw + hj), colP(fcw + NH + hj), colP(fcw + 2 * NH + hj), colP(fcb + hj)
                        if ti < 2:
                            cur = gbufs[1 + ti]
                            prev = gbufs[ti]
                            CP("act", grj[:, 2 + c0:2 + c0 + n], gb[:, :n], [gbb], [cur])
                            ACT(aa[:, :n], gb[:, :n], AF.Identity, [gbb, b_pcol], [aab], bias=bc, scale=w2)
                            STT(aa[:, :n], grj[:, 1 + c0:1 + c0 + n], w1, aa[:, :n], ALU.mult, ALU.add,
                                [cur, prev, aab, b_pcol], [aab])
                            STT(aa[:, :n], grj[:, c0:c0 + n], w0, aa[:, :n], ALU.mult, ALU.add, [cur, prev, aab, b_pcol],
                                [aab])
                            if ti == 1:
                                if st == 0:
                                    CP("act", halo_ffn[:, l, hj, :], grj[:, 1024:1026], [cur], [b_halo_ffn[l][hj]])
                                else:
                                    CP("act", CFP[:, hj, :], grj[:, 1024:1026], [cur], [b_CFP])
                        else:
                            cfv = CF[:, hj, :].rearrange("p (s k) -> p s k", k=2)
                            CP("act", NEWG[:, hj, :], gb[:, :n], [gbb], [b_NEWG])
                            ACT(aa[:, :n], gb[:, :n], AF.Identity, [gbb, b_pcol], [aab], bias=bc, scale=w2)
                            STT(aa[:, :n], cfv[:, :, 1], w1, aa[:, :n], ALU.mult, ALU.add, [b_CF, aab, b_pcol], [aab])
                            STT(aa[:, :n], cfv[:, :, 0], w0, aa[:, :n], ALU.mult, ALU.add, [b_CF, aab, b_pcol], [aab])
                        ACT(sg[:, :n], aa[:, :n], AF.Silu, [aab], [sgb])
                        TT("dve", HID[:, hj, c0:c0 + n], sg[:, :n], ub[:, :n], ALU.mult, [sgb, ubb], [hidb[hj][ti]])
        if st == 1:
            for kk in range(2):
                kb.dma("sp", D["cfp"][l][kk].rearrange("(j p) -> p j", p=128), CFP[:, :, kk], reads=[b_CFP],
                       allow_slow_non_contiguous=True)
            cfs_v = D["cfs"][l].rearrange("(s k) c -> s k c", k=2)
            cfc_v = D["cfc"][l].rearrange("(s k) c -> s k c", k=2)
            kb.dma("sp", cfs_v[:, 0, :], cfc_v[:, 1, :])
            store_rows_T(lambda k: NEWG[:, k, :], lambda k: [b_NEWG], NS, DFF, cfs_v[:, 1, :], stage=(bigst, b_bigst))
        if mid_hook is not None:
            mid_hook()
        tl = tiles_of(st)
        passes = [[tl[0]], tl[1:]]
        for ps_ in passes:
            for m in range(8):
                dv, dch = load_w2(wd[:, m * 128:(m + 1) * 128], NH, 128)
                for (c0, n, ti) in ps_:
                    bk, bb = bank()
                    for k in range(NH):
                        MM(bk[:, :n], dv[:, k, :], HID[:, k, c0:c0 + n], k == 0, k == NH - 1,
                           [wb_of(dch, k), hidb[k][ti]], [bb])
                    STT(X32[:, m, c0:c0 + n], X32[:, m, c0:c0 + n], ALPHA, bk[:, :n], ALU.mult, ALU.add,
                        [bb, x32b[m][ti]], [x32b[m][ti]])
            residual_ln(st, l, "ffn", ps_)
        AR.release(m0)

    def mixer_odd(st):
        m0 = AR.mark()
        W = D["w_in_c"]
        GB, rGB = AR.bf16([128, 8, 30 + NCOL])
        gbb_ = [[nbuf(rGB) for _ in range(3)] for _ in range(8)]
        b_gbh = [nbuf(rGB) for _ in range(8)]
        H32, rH = AR.f32([128, 8, NCOL])
        h32b = [[nbuf(rH) for _ in range(3)] for _ in range(8)]
        SGT = [AR.f32([128, 512]) for _ in range(2)]
        SGT = [(a, nbuf(r)) for a, r in SGT]
        LWs = [AR.bf16([128, 32, 32]) for _ in range(2)]
        LWs = [(a, nbuf(r)) for a, r in LWs]
        GSW = 1052
        GSs = []
        for _ in range(3):
            ap_, r_ = AR.bf16([128, 4, GSW])
            bufs_ = [[nbuf(r_) for _j in range(4)] for _q in range(4)]
            MEMSET("dve", ap_, 0.0, [b_ for row in bufs_ for b_ in row])
            GSs.append((ap_, bufs_))
        rr = [0]
        m_tmp = AR.mark()
        if st == 1:
            G32T, r_ = AR.f32([128, 8, 30]); b_G32T = nbuf(r_)
            GS32, r_ = AR.f32([128, 8, NS]); b_GS32 = nbuf(r_)
            CC, r_ = AR.f32([128, 8, NS * 30]); b_CC = nbuf(r_)
            CT, r_ = AR.f32([128, NS * 30]); b_CT = nbuf(r_)
            RED, r_ = AR.f32([128, NS]); b_RED = nbuf(r_)
            for q in range(4):
                load_rows_T(D["ccc"][q * 120:(q + 1) * 120, :], 120, 1024,
                            lambda k0, nk, q=q: [("act", CC[:, k0:k0 + nk, q * 120:(q + 1) * 120], [b_CC])], None)
        for ct in range(8):
            CP("act", GB[:, ct, 0:30], halo_g[:, ct, :], [b_halo_g], [b_gbh[ct]])
        def emit_gs(ct):
            if ct >= 8:
                return
            gs, gsb = GSs[ct % 3]
            gsd, gsdb = GSD[ct % 3]
            kb.dma("sp", gsd[:, 0:1054], GB[:, ct, 0:1054], reads=[b_gbh[ct], gbb_[ct][0], gbb_[ct][1]], writes=[gsdb])
            for j in range(4):
                wd_ = min(GSW, 1054 - j)
                kb.dma("sp", gs[32 * j:32 * j + 32, :, 0:wd_],
                       gsd[:, j:j + wd_].rearrange("(q c) x -> c q x", c=32),
                       reads=[gsdb], writes=[gsb[q_][j] for q_ in range(4)])

        for un in range(4):
            if un == 1:
                emit_gs(0)
                emit_gs(1)
            if un == 2:
                emit_gs(2)
            sv, (ach, bch) = load_wm([(W[:, un * 256:(un + 1) * 256], 256), (W[:, 1024 + un * 256:1024 + (un + 1) * 256], 256)], 8)
            for (c0, n, ti) in tiles_of(st):
                for j in range(2):
                    ct = un * 2 + j
                    bb_, bbb = bank()
                    ab, abb = bank()
                    for k in range(8):
                        MM(bb_[:, :n], sv[:, k, 256 + j * 128:256 + (j + 1) * 128], XB[:, k, c0:c0 + n], k == 0, k == 7,
                           [bch, xbb[k][ti]], [bbb])
                    for k in range(8):
                        MM(ab[:, :n], sv[:, k, j * 128:(j + 1) * 128], XB[:, k, c0:c0 + n], k == 0, k == 7,
                           [ach, xbb[k][ti]], [abb])
                    i = rr[0]
                    rr[0] = (i + 1) % 2
                    sgt, sgtb = SGT[i]
                    ACT(sgt[:, :n], bb_[:, :n], AF.Sigmoid, [bbb], [sgtb])
                    if ti < 2:
                        TT("dve", GB[:, ct, 30 + c0:30 + c0 + n], ab[:, :n], sgt[:, :n], ALU.mult, [abb, sgtb],
                           [gbb_[ct][ti]])
                        if st == 1 and ti == 1:
                            TT("dve", G32T[:, ct, :], ab[:, n - 30:n], sgt[:, n - 30:n], ALU.mult, [abb, sgtb],
                               [b_G32T])
                    else:
                        TT("dve", GS32[:, ct, :], ab[:, :n], sgt[:, :n], ALU.mult, [abb, sgtb], [b_GS32])
        if st == 0:
            for ct in range(8):
                CP("act", halo_g[:, ct, :], GB[:, ct, 1024:1054], [gbb_[ct][1]], [b_halo_g])
        cw = PC["convcw"]
        if st == 1:
            c0, n, ti = tiles_of(st)[2]
            for ct in range(8):
                ccv = CC[:, ct, :].rearrange("p (s k) -> p s k", k=30)
                wrow = pcol[:, cw + ct:cw + ct + 30 * 8:8]
                TT("dve", CT.rearrange("p (s k) -> p s k", k=30), ccv, wrow.unsqueeze(1).to_broadcast([128, NS, 30]),
                   ALU.mult, [b_CC, b_pcol], [b_CT])
                kb.op("dve", lambda e: e.tensor_reduce(RED, CT.rearrange("p (s k) -> p s k", k=30), AX.X, ALU.add),
                      [b_CT], [b_RED])
                STT(RED, GS32[:, ct, :], colP(cw + 30 * 8 + ct), RED, ALU.mult, ALU.add, [b_GS32, b_RED, b_pcol],
                    [b_RED])
                ACT(H32[:, ct, c0:c0 + n], RED, AF.Identity, [b_RED, b_pcol], [h32b[ct][ti]],
                    bias=colP(PC["convcb"] + ct))
            store_rows_T(lambda k: G32T[:, k, :], lambda k: [b_G32T], 30, 1024, D["ccp"][:, :])
            ccs_v = D["ccs"].rearrange("(s k) c -> s k c", k=30)
            ccc_v = D["ccc"].rearrange("(s k) c -> s k c", k=30)
            kb.dma("sp", ccs_v[:, 0:29, :], ccc_v[:, 1:30, :])
            store_rows_T(lambda k: GS32[:, k, :], lambda k: [b_GS32], NS, 1024, ccs_v[:, 29, :])
        AR.release(m_tmp)
        HB, rHB = AR.bf16([128, 8, NCOL])
        hbb = [[nbuf(rHB) for _ in range(3)] for _ in range(8)]
        wblk = [load_w2(D["w_out_c"][:, blk * 512:(blk + 1) * 512], 8, 512) for blk in range(2)]

        def part_A(c0, n, ti):
            ln_feat(n, 8, lambda k: H32[:, k, c0:c0 + n], lambda k: h32b[k][ti],
                    lambda k: colP(PC["lncg"] + k), lambda k: colP(PC["lncb"] + k),
                    None, None, lambda k: HB[:, k, c0:c0 + n], lambda k: hbb[k][ti], func=AF.Silu,
                    src4=lambda k0: H32[:, k0:k0 + 4, c0:c0 + n])

        def part_B(c0, n, ti):
            for blk in range(2):
                wv, wch = wblk[blk]
                for m in range(4):
                    bk, bb = bank()
                    for k in range(8):
                        MM(bk[:, :n], wv[:, k, m * 128:(m + 1) * 128], HB[:, k, c0:c0 + n], k == 0, k == 7,
                           [wb_of(wch, k), hbb[k][ti]], [bb])
                    km = blk * 4 + m
                    STT(X32[:, km, c0:c0 + n], X32[:, km, c0:c0 + n], ALPHA, bk[:, :n], ALU.mult, ALU.add,
                        [bb, x32b[km][ti]], [x32b[km][ti]])

        def part_C(c0, n, ti):
            residual_ln(st, 1, "mix", [(c0, n, ti)])

        tl = tiles_of(st)
        t0_, t1_ = tl[0], tl[1]
        for ct in range(8):
            i = rr[0]
            rr[0] = (i + 1) % 2
            lw, lwb = LWs[i]
            gs, gsb = GSs[ct % 3]
            TT("dve", lw, m32.unsqueeze(1).to_broadcast([128, 32, 32]),
               convst[:, ct * 32:(ct + 1) * 32].unsqueeze(2).to_broadcast([128, 32, 32]), ALU.mult,
               [b_m32, b_convst], [lwb])
            for (c0, n, ti) in (t0_, t1_):
                bk, bb = bank()
                for tg in range(8):
                    for q in range(4):
                        kb.op("pe", lambda e, bk=bk, q=q, tg=tg, lw=lw, gs=gs, c0=c0, n=n: e.matmul(
                            bk[32 * q:32 * q + 32, :n], lw[:, q * 8 + tg, :], gs[:, q, c0 + 4 * tg:c0 + 4 * tg + n],
                            start=(tg == 0), stop=(tg == 7), tile_position=(0, 32 * q)),
                            [lwb] + gsb[q], [bb])
                ACT(H32[:, ct, c0:c0 + n], bk[:, :n], AF.Identity, [bb, b_pcol], [h32b[ct][ti]],
                    bias=colP(PC["convcb"] + ct))
            emit_gs(ct + 3)
        part_A(*t0_)
        part_A(*t1_)
        part_B(*t0_)
        part_C(*t0_)
        if st == 1:
            part_A(*tl[2])
        part_B(*t1_)
        part_C(*t1_)
        if st == 1:
            part_B(*tl[2])
            part_C(*tl[2])
        AR.release(m0)

    def mixer_even(st, pre_s5_hook=None):
        m0 = AR.mark()
        W = D["w_in_ab"]
        nchunk = 8
        YM, rYM = AR.bf16([128, 8, NCOL])
        ymb = [[nbuf(rYM) for _ in range(3)] for _ in range(8)]
        UAB, rUA = AR.bf16([128, 4, NCOL])
        uab = [[nbuf(rUA) for _ in range(3)] for _ in range(4)]
        r_COS0 = AR.mark()
        COS, r_ = AR.f32([128, 16, LCH]); b_COS = nbuf(r_)
        SIN, r_ = AR.f32([128, 16, LCH]); b_SIN = nbuf(r_)
        RT, r_ = AR.f32([128, 16, LCH]); b_RT = nbuf(r_)
        SM, r_ = AR.f32([128, 24, 16]); b_SM = nbuf(r_)
        cw_m = AR.mark()
        BTR, r_ = AR.bf16([128, 16, 128]); b_BTR = nbuf(r_)
        BTI, r_ = AR.bf16([128, 16, 128]); b_BTI = nbuf(r_)
        CR, r_ = AR.bf16([128, 16, 128]); b_CR = nbuf(r_)
        NCR, r_ = AR.bf16([128, 16, 128]); b_NCR = nbuf(r_)
        NCI, r_ = AR.bf16([128, 16, 128]); b_NCI = nbuf(r_)
        DD, r_ = AR.bf16([128, 4, 128]); b_DD = nbuf(r_)
        (S_DT, S_A, S_TH, S_R, S_F, S_LBR, S_LBI, S_CLR, S_SLR, S_KRE, S_KIM, S_T0, S_T1, S_T2, S_T3, S_LRE, S_LIM,
         S_CL1, S_SL1) = range(19)

        def sm(i):
            return SM[:, i, :]
        const_bufs = [b_COS, b_SIN, b_RT, b_BTR, b_BTI, b_CR, b_NCR, b_NCI, b_DD, b_SM]
        cf32_bufs = [b_COS, b_SIN, b_RT, b_SM]
        cbf_bufs = [b_BTR, b_BTI, b_CR, b_NCR, b_NCI, b_DD]
        cw_a, cw_b = r_COS0, AR.mark()
        def s5_setup_compute():
            MEMSET("dve", AR_t[:, cw_a:cw_b], 0.0, const_bufs)
            m2 = AR.mark()
            LR, r_ = AR.f32([128, 16]); b_LR = nbuf(r_)
            LI, r_ = AR.f32([128, 16]); b_LI = nbuf(r_)
            LD, r_ = AR.f32([128, 16]); b_LD = nbuf(r_)
            TAUI, r_ = AR.f32([128, LCH]); b_TAUI = nbuf(r_)
            TAU, r_ = AR.f32([128, LCH]); b_TAU = nbuf(r_)
            XR = rX32[0]

            def xr_buf(k_):
                r__ = AR.at(XR + 2048 * k_, 2048)
                return AR_t[:, XR + 2048 * k_:XR + 2048 * (k_ + 1)].rearrange("p (a b) -> p a b", a=16), nbuf(r__)
            PH, b_PH = xr_buf(0)
            BX1, b_BX1 = xr_buf(1)
            BX2, b_BX2 = xr_buf(2)
            BT2, b_BT2 = xr_buf(3)
            BT1, b_BT1 = PH, b_PH
            kb.dma("sp", LR, D["lamre"], writes=[b_LR])
            kb.dma("sp", LI, D["lamim"], writes=[b_LI])
            kb.dma("sp", LD, D["logdt"], writes=[b_LD])
            kb.dma("pool", CR, D["crex"], writes=[b_CR])
            kb.dma("pool", NCI, D["cimx"], writes=[b_NCI])
            TS("dve", NCR, CR, -1.0, None, ALU.mult, None, [b_CR], [b_NCR])
            TS("dve", NCI, NCI, -1.0, None, ALU.mult, None, [b_NCI], [b_NCI])
            for ct in range(4):
                TS("dve", DD[:, ct, :], identb, colP(PC["s5d"] + ct), None, ALU.mult, None, [b_identb, b_pcol], [b_DD])
            S = [b_SM]
            ACT(sm(S_DT), LD, AF.Exp, [b_LD], S)
            TT("dve", sm(S_A), LR, sm(S_DT), ALU.mult, [b_LR] + S, S)
            TT("dve", sm(S_TH), LI, sm(S_DT), ALU.mult, [b_LI] + S, S)
            ACT(sm(S_R), sm(S_A), AF.Exp, S, S)
            SMI, r_ = AR.f32([128, 16]); b_SMI = nbuf(r_)

            def frac_sym(x, xb_, ti, tib, tf, tfb):
                CP("dve", ti, x, [xb_], [tib])
                CP("dve", tf, ti, [tib], [tfb])
                TT("dve", x, x, tf, ALU.subtract, [xb_, tfb], [xb_])

            def sincos(dst_sin, dsb, dst_cos, dcb, ph, phb, ti, tib, tf, tfb):
                frac_sym(ph, phb, ti, tib, tf, tfb)
                ACT(dst_sin, ph, AF.Sin, [phb], [dsb], scale=2.0 * math.pi)
                TS("dve", ph, ph, 0.25, None, ALU.add, None, [phb], [phb])
                frac_sym(ph, phb, ti, tib, tf, tfb)
                ACT(dst_cos, ph, AF.Sin, [phb], [dcb], scale=2.0 * math.pi)
            TS("dve", sm(S_F), sm(S_TH), 1.0 / (2.0 * math.pi), None, ALU.mult, None, S, S)
            frac_sym(sm(S_F), b_SM, SMI.bitcast(I32), b_SMI, sm(S_T0), b_SM)
            kb.op("pool", lambda e: e.iota(TAUI.bitcast(I32), pattern=[[1, LCH]], base=0, channel_multiplier=0), (), [b_TAUI])
            CP("dve", TAU, TAUI.bitcast(I32), [b_TAUI], [b_TAU])
            TT("dve", PH, sm(S_F).unsqueeze(2).to_broadcast([128, 16, LCH]), TAU.unsqueeze(1).to_broadcast([128, 16, LCH]),
               ALU.mult, S + [b_TAU], [b_PH])
            sincos(SIN, b_SIN, COS, b_COS, PH, b_PH, BX1.bitcast(I32), b_BX1, BX2, b_BX2)
            TS("dve", sm(S_T0), sm(S_F), float(LCH), None, ALU.mult, None, S, S)
            sincos(sm(S_SLR), b_SM, sm(S_CLR), b_SM, sm(S_T0), b_SM, SMI.bitcast(I32), b_SMI, sm(S_T1), b_SM)
            TT("dve", sm(S_CLR), sm(S_CLR), sm(S_R), ALU.mult, S, S)
            TT("dve", sm(S_SLR), sm(S_SLR), sm(S_R), ALU.mult, S, S)
            TT("dve", sm(S_LBR), COS[:, :, 1], sm(S_R), ALU.mult, S + [b_COS], S)
            TT("dve", sm(S_LBI), SIN[:, :, 1], sm(S_R), ALU.mult, S + [b_SIN], S)
            CP("dve", sm(S_CL1), COS[:, :, LCH - 1], [b_COS], S)
            CP("dve", sm(S_SL1), SIN[:, :, LCH - 1], [b_SIN], S)
            TS("dve", sm(S_T0), sm(S_LBR), -1.0, None, ALU.add, None, S, S)
            TT("dve", sm(S_T1), LR, LR, ALU.mult, [b_LR], S)
            TT("dve", sm(S_T2), LI, LI, ALU.mult, [b_LI], S)
            TT("dve", sm(S_T1), sm(S_T1), sm(S_T2), ALU.add, S, S)
            kb.op("dve", lambda e: e.reciprocal(sm(S_T1), sm(S_T1)), S, S)
            TT("dve", sm(S_T2), sm(S_T0), LR, ALU.mult, S + [b_LR], S)
            TT("dve", sm(S_T3), sm(S_LBI), LI, ALU.mult, S + [b_LI], S)
            TT("dve", sm(S_T2), sm(S_T2), sm(S_T3), ALU.add, S, S)
            TT("dve", sm(S_KRE), sm(S_T2), sm(S_T1), ALU.mult, S, S)
            TT("dve", sm(S_T2), sm(S_LBI), LR, ALU.mult, S + [b_LR], S)
            TT("dve", sm(S_T3), sm(S_T0), LI, ALU.mult, S + [b_LI], S)
            TT("dve", sm(S_T2), sm(S_T2), sm(S_T3), ALU.subtract, S, S)
            TT("dve", sm(S_KIM), sm(S_T2), sm(S_T1), ALU.mult, S, S)
            CP("dve", RT, sm(S_R).unsqueeze(2).to_broadcast([128, 16, LCH]), S, [b_RT])
            MEMSET("dve", RT[:, :, 0:1], 0.0, [b_RT])
            kb.dma("sp", BX1, D["brex"], writes=[b_BX1])
            kb.dma("sp", BX2, D["bimx"], writes=[b_BX2])
            kre_b = sm(S_KRE).unsqueeze(2).to_broadcast([128, 16, 128])
            kim_b = sm(S_KIM).unsqueeze(2).to_broadcast([128, 16, 128])
            TT("dve", BT1, BX1, kre_b, ALU.mult, [b_BX1] + S, [b_BT1])
            TT("dve", BT2, BX2, kim_b, ALU.mult, [b_BX2] + S, [b_BT2])
            TT("dve", BT1, BT1, BT2, ALU.subtract, [b_BT1, b_BT2], [b_BT1])
            TT("dve", BT2, BX2, kre_b, ALU.mult, [b_BX2] + S, [b_BT2])
            TT("dve", BX1, BX1, kim_b, ALU.mult, [b_BX1] + S, [b_BX1])
            TT("dve", BT2, BT2, BX1, ALU.add, [b_BT2, b_BX1], [b_BT2])
            for (src, sb_, dst, db) in ((BT1, b_BT1, BTR, b_BTR), (BT2, b_BT2, BTI, b_BTI)):
                for p0 in range(0, 16, 4):
                    bk, bb = bank()
                    for j in range(4):
                        TR(bk[:, j * 128:(j + 1) * 128], src[:, p0 + j, :], ident, [sb_, b_ident], [bb])
                    CP("act", dst[:, p0:p0 + 4, :], bk.rearrange("p (a b) -> p a b", a=4), [bb], [db])
            AR.release(m2)
            kb.dma("sp", S5C[:, 0:cw_m - cw_a], AR_t[:, cw_a:cw_m], reads=cf32_bufs, writes=[b_S5C])
            kb.dma("sp", S5CB[:, 0:2 * (cw_b - cw_m)], AR_t[:, cw_m:cw_b].bitcast(BF16), reads=cbf_bufs, writes=[b_S5CB])
        if st == 1:
            kb.dma("sp", AR_t[:, cw_a:cw_m], S5C[:, 0:cw_m - cw_a], reads=[b_S5C], writes=cf32_bufs)
            kb.dma("sp", AR_t[:, cw_m:cw_b].bitcast(BF16), S5CB[:, 0:2 * (cw_b - cw_m)], reads=[b_S5CB], writes=cbf_bufs)
        m1 = AR.mark()
        VNT, rV = AR.bf16([128, nchunk + 1, 512])
        vntb = [nbuf(rV) for _ in range(nchunk + 1)]
        LNG, r_ = AR.f32([128, 512]); b_LNG = nbuf(r_)
        LNB, r_ = AR.f32([128, 512]); b_LNB = nbuf(r_)
        WST32, r_ = AR.f32([128, 4, 128]); b_WST32 = nbuf(r_)
        WST, r_ = AR.bf16([128, 4, 128]); b_WST = nbuf(r_)
        SGB, r_ = AR.f32([128, 4, 128]); b_SGB = nbuf(r_)
        VT = [AR.f32([128, 512]) for _ in range(1)] * 2
        VT = [(a, nbuf(r)) for a, r in VT[:1]] * 2
        VN32 = [AR.f32([128, 512]) for _ in range(1)]
        VN32 = [(a, nbuf(r)) for a, r in VN32] * 2
        BNS, r_ = AR.f32([128, 16]); b_BNS = nbuf(r_)
        GATE = [AR.f32([128, 512]) for _ in range(1)]
        GATE = [(a, nbuf(r)) for a, r in GATE] * 2
        kb.dma("sp", LNG, D["sglng"], writes=[b_LNG])
        kb.dma("sp", LNB, D["sglnb"], writes=[b_LNB])
        kb.dma("sp", WST32, D["sguwT"], writes=[b_WST32])
        kb.dma("sp", SGB, D["sgubc"], writes=[b_SGB])
        for h in range(4):
            kb.op("pool", lambda e, h=h: e.affine_select(WST32[:, h, :], WST32[:, h, :], pattern=[[1, 128]],
                                                         compare_op=ALU.is_ge, fill=0.0, base=0, channel_multiplier=-1),
                  [b_WST32], [b_WST32])
        CP("dve", WST, WST32, [b_WST32], [b_WST])
        if st == 1:
            W00, r_ = AR.f32([NS, 512]); b_W00 = nbuf(r_)
            B0, r_ = AR.f32([NS, 512]); b_B0 = nbuf(r_)
            GTOK, r_ = AR.f32([NS, 512]); b_GTOK = nbuf(r_)
            GSS, r_ = AR.f32([128, 4, NS]); b_GSS = nbuf(r_)
            kb.dma("sp", W00, D["sgw00"], writes=[b_W00])
            kb.dma("sp", B0, D["sgb0"], writes=[b_B0])
        vv, vch = load_w2(W[:, 1024:1536], 8, 512)
        rr = [0]
        chunks = [(c * 128, 128, c, c // 4) for c in range(nchunk)]
        if st == 1:
            chunks.append((1024, NS, nchunk, 2))
        SD = 6
        for (c0, M, ci, ti) in chunks:
            bk, bb = bank()
            for k in range(8):
                MM(bk[:M, :], XB[:, k, c0:c0 + M], vv[:, k, :], k == 0, k == 7, [wb_of(vch, k), xbb[k][ti]], [bb])
            i = rr[0]
            rr[0] = (i + 1) % 2
            (vt, vtb), (vn, vnb) = VT[i], VN32[i]
            kb.op("dve", lambda e, bk=bk, M=M: e.bn_stats(BNS[:M, 0:SD], bk[:M, :]), [bb], [b_BNS])
            kb.op("dve", lambda e, M=M: e.bn_aggr(BNS[:M, 8:10], BNS[:M, 0:SD]), [b_BNS], [b_BNS])
            TS("dve", BNS[:M, 10:11], BNS[:M, 9:10], 0.0, LN_EPS, ALU.max, ALU.add, [b_BNS], [b_BNS])
            ACT(BNS[:M, 10:11], BNS[:M, 10:11], AF.Ln, [b_BNS], [b_BNS])
            ACT(BNS[:M, 11:12], BNS[:M, 10:11], AF.Exp, [b_BNS], [b_BNS], scale=-0.5)
            TS("dve", vt[:M, :], bk[:M, :], BNS[:M, 8:9], BNS[:M, 11:12], ALU.subtract, ALU.mult, [bb, b_BNS], [vtb])
            TT("dve", vt[:M, :], vt[:M, :], LNG[:M, :], ALU.mult, [vtb, b_LNG], [vtb])
            TT("dve", vn[:M, :], vt[:M, :], LNB[:M, :], ALU.add, [vtb, b_LNB], [vnb])
            CP("act", VNT[:M, ci, :], vn[:M, :], [vnb], [vntb[ci]])
            if ci == nchunk:
                kb.dma("sp", D["sbv"][:, :], vn[:M, :], reads=[vnb])
                TT("dve", GTOK, vn[:M, :], W00, ALU.mult, [vnb, b_W00], [b_GTOK])
                TT("dve", GTOK, GTOK, B0, ALU.add, [b_GTOK, b_B0], [b_GTOK])
                bk2, bb2 = bank()
                for h in range(4):
                    TR(bk2[:, h * NS:(h + 1) * NS], GTOK[:NS, h * 128:(h + 1) * 128], ident[:NS, :NS],
                       [b_GTOK, b_ident], [bb2])
                CP("act", GSS, bk2[:, 0:4 * NS].rearrange("p (a b) -> p a b", a=4), [bb2], [b_GSS])
        uv, uch = load_w2(W[:, 512:1024], 8, 512)
        for (c0, n, ti) in tiles_of(st):
            for h in range(4):
                ub, ubb = bank()
                for k in range(8):
                    MM(ub[:, :n], uv[:, k, h * 128:(h + 1) * 128], XB[:, k, c0:c0 + n], k == 0, k == 7,
                       [wb_of(uch, k), xbb[k][ti]], [ubb])
                if ti < 2:
                    gk, gkb = bank()
                    for c in range(4):
                        ci = ti * 4 + c
                        MM(gk[:, c * 128:(c + 1) * 128], VNT[:, ci, h * 128:(h + 1) * 128], WST[:, h, :], True, True,
                           [vntb[ci], b_WST], [gkb])
                    i = rr[0]
                    rr[0] = (i + 1) % 2
                    ga, gab = GATE[i]
                    TT("dve", ga.rearrange("p (a b) -> p a b", a=4), gk.rearrange("p (a b) -> p a b", a=4),
                       SGB[:, h, :].unsqueeze(1).to_broadcast([128, 4, 128]), ALU.add, [gkb, b_SGB], [gab])
                    TT("dve", YM[:, 4 + h, c0:c0 + n], ub[:, :n], ga[:, :n], ALU.mult, [ubb, gab], [ymb[4 + h][ti]])
                else:
                    TT("dve", YM[:, 4 + h, c0:c0 + n], ub[:, :n], GSS[:, h, :], ALU.mult, [ubb, b_GSS], [ymb[4 + h][ti]])
        av, ach = load_w2(W[:, 0:512], 8, 512)
        for (c0, n, ti) in tiles_of(st):
            for m in range(4):
                bk, bb = bank()
                for k in range(8):
                    MM(bk[:, :n], av[:, k, m * 128:(m + 1) * 128], XB[:, k, c0:c0 + n], k == 0, k == 7,
                       [wb_of(ach, k), xbb[k][ti]], [bb])
                CP("act", UAB[:, m, c0:c0 + n], bk[:, :n], [bb], [uab[m][ti]])
        AR.release(m1)
        if st == 0:
            s5_setup_compute()
        if pre_s5_hook is not None:
            pre_s5_hook()
        glv, glch = load_w2(D["s5_glu_w"], 4, 512)
        X32_WORD0 = rX32[0]
        WZW = 2 * 16 * LCH
        rW0 = AR.at(XB_WORD0, XB_WORDS)
        rW1 = AR.at(X32_WORD0, WZW)
        WZs = [AR_t[:, XB_WORD0:XB_WORD0 + WZW].rearrange("p (c a b) -> p c a b", c=2, a=16),
               AR_t[:, X32_WORD0:X32_WORD0 + WZW].rearrange("p (c a b) -> p c a b", c=2, a=16)]
        b_Ws = [[nbuf(rW0), nbuf(rW0)], [nbuf(rW1), nbuf(rW1)]]
        xoff = [X32_WORD0 + WZW]
        scratch_bufs = [b_Ws[1][0], b_Ws[1][1]]

        def xalloc(words, shape, dt):
            w0_ = xoff[0]
            xoff[0] += (words + 7) // 8 * 8
            assert xoff[0] <= rX32[1]
            rec = AR.at(w0_, words)
            ap_ = AR_t[:, w0_:w0_ + words]
            if dt == BF16:
                ap_ = ap_.bitcast(BF16)
            bb_ = nbuf(rec)
            scratch_bufs.append(bb_)
            return Arena._shape(ap_, shape), bb_
        TM = [AR.f32([128, 512]) for _ in range(8)]
        TM = [(a_, nbuf(r_)) for a_, r_ in TM]
        ADD_ENG = "pool"
        def aalloc(shape, dt):
            ap_, r_ = (AR.bf16(shape) if dt == BF16 else AR.f32(shape))
            return ap_, nbuf(r_)
        PP = [aalloc([128, 4, 512], BF16) for _ in range(2)] + [xalloc(1024, [128, 4, 512], BF16) for _ in range(2)]
        PPK = []
        for (pp_, ppb_) in PP:
            kbufs = [Buf("ppk") for _ in range(4)]
            for kb_ in kbufs:
                inherit(kb_, [ppb_])
                scratch_bufs.append(kb_)
            PPK.append(kbufs)
        DEMOD_POOL = set()
        GAs = [(aalloc([128, 4, 128], F32), aalloc([128, 4, 128], BF16), aalloc([128, 4, 128], F32)),
               (xalloc(512, [128, 4, 128], F32), xalloc(256, [128, 4, 128], BF16), xalloc(512, [128, 4, 128], F32))]
        ZL, r_ = AR.f32([128, 4, 16]); b_ZL = nbuf(r_)
        prr = [0]

        def glu_front(ybank, ybb, n, gset, zbank=None):
            (GA32, b_GA32), (GAB, b_GAB), (SGG, b_SGG) = gset
            yv = ybank[:, 0:4 * n].rearrange("p (a b) -> p a b", a=4)
            ACT(GA32[:, :, :n], yv, AF.Gelu_apprx_tanh, [ybb], [b_GA32])
            CP("act", GAB[:, :, :n], GA32[:, :, :n], [b_GA32], [b_GAB])
            if zbank is None:
                zb, zbb = bank()
            else:
                zb, zbb = banks[zbank][:, :], bank_buf[zbank]
            for m in range(4):
                for k in range(4):
                    MM(zb[:, m * n:(m + 1) * n], glv[:, k, m * 128:(m + 1) * 128], GAB[:, k, :n], k == 0, k == 3,
                       [wb_of(glch, k), b_GAB], [zbb])
            for m in range(4):
                ACT(SGG[:, m, :n], zb[:, m * n:(m + 1) * n], AF.Sigmoid, [zbb, b_pcol], [b_SGG],
                    bias=colP(PC["glub"] + m))

        def glu_back(c0, n, ti, gset):
            (GA32, b_GA32), (GAB, b_GAB), (SGG, b_SGG) = gset
            for m in range(4):
                TT("dve", YM[:, m, c0:c0 + n], GA32[:, m, :n], SGG[:, m, :n], ALU.mult, [b_GA32, b_SGG], [ymb[m][ti]])

        def emit_bu(job):
            if job >= nchunk * 4:
                return
            c, grp = divmod(job, 4)
            c0 = c * LCH
            ti = c // 4
            are, areb = banks[(job % 2) * 2][:, :], bank_buf[(job % 2) * 2]
            aim, aimb = banks[(job % 2) * 2 + 1][:, :], bank_buf[(job % 2) * 2 + 1]
            for j in range(4):
                p = grp * 4 + j
                MM(are[:, j * 128:(j + 1) * 128], BTR[:, p, :], UAB[:, p // 4, c0:c0 + LCH], True, True,
                   [b_BTR, uab[p // 4][ti]], [areb])
                MM(aim[:, j * 128:(j + 1) * 128], BTI[:, p, :], UAB[:, p // 4, c0:c0 + LCH], True, True,
                   [b_BTI, uab[p // 4][ti]], [aimb])

        def stage_M(c):
            WZ, b_W = WZs[c % 2], b_Ws[c % 2]
            c0 = c * LCH
            ti = c // 4
            first = (st == 0 and c == 0)
            for grp in range(4):
                job = c * 4 + grp
                are, areb = banks[(job % 2) * 2][:, :], bank_buf[(job % 2) * 2]
                aim, aimb = banks[(job % 2) * 2 + 1][:, :], bank_buf[(job % 2) * 2 + 1]
                cs = COS[:, grp * 4:(grp + 1) * 4, :].rearrange("p a b -> p (a b)")
                sn = SIN[:, grp * 4:(grp + 1) * 4, :].rearrange("p a b -> p (a b)")
                (t1, t1b), (t2, t2b), (t3, t3b), (t4, t4b) = TM[(job % 2) * 4:(job % 2) * 4 + 4]
                wre = WZ[:, 0, grp * 4:(grp + 1) * 4, :].rearrange("p a b -> p (a b)")
                wim = WZ[:, 1, grp * 4:(grp + 1) * 4, :].rearrange("p a b -> p (a b)")
                TT("dve", t1, are, cs, ALU.mult, [areb, b_COS], [t1b])
                TT("dve", t2, aim, sn, ALU.mult, [aimb, b_SIN], [t2b])
                TT("dve", t3, aim, cs, ALU.mult, [aimb, b_COS], [t3b])
                TT("dve", t4, are, sn, ALU.mult, [areb, b_SIN], [t4b])
                emit_bu(job + 2)
                TT(ADD_ENG, wre, t1, t2, ALU.add, [t1b, t2b], [b_W[0]])
                TT(ADD_ENG, wim, t3, t4, ALU.subtract, [t3b, t4b], [b_W[1]])
            if not first:
                TT("dve", WZ[:, 0, :, 0], WZ[:, 0, :, 0], zcar[:, 0, :], ALU.add, [b_W[0], b_zcar], [b_W[0]])
                TT("dve", WZ[:, 1, :, 0], WZ[:, 1, :, 0], zcar[:, 1, :], ALU.add, [b_W[1], b_zcar], [b_W[1]])
            for ri in range(2):
                flat = WZ[:, ri, :, :].rearrange("p a b -> p (a b)")
                kb.op("dve", lambda e, flat=flat: e.tensor_tensor_scan(flat, RT.rearrange("p a b -> p (a b)"), flat, 0.0,
                                                                       ALU.mult, ALU.add),
                      [b_RT, b_W[ri]], [b_W[ri]])
            zlr, zli = WZ[:, 0, :, LCH - 1], WZ[:, 1, :, LCH - 1]
            CARRY_ENG = "pool"
            TT(CARRY_ENG, ZL[:, 0, :], zlr, sm(S_CLR), ALU.mult, [b_W[0], b_SM], [b_ZL])
            TT(CARRY_ENG, ZL[:, 1, :], zli, sm(S_SLR), ALU.mult, [b_W[1], b_SM], [b_ZL])
            TT(CARRY_ENG, ZL[:, 2, :], zlr, sm(S_SLR), ALU.mult, [b_W[0], b_SM], [b_ZL])
            TT(CARRY_ENG, ZL[:, 3, :], zli, sm(S_CLR), ALU.mult, [b_W[1], b_SM], [b_ZL])
            TT(CARRY_ENG, zcar[:, 0, :], ZL[:, 0, :], ZL[:, 1, :], ALU.subtract, [b_ZL], [b_zcar])
            TT(CARRY_ENG, zcar[:, 1, :], ZL[:, 2, :], ZL[:, 3, :], ALU.add, [b_ZL], [b_zcar])
            if st == 1 and c == nchunk - 1:
                FS, r_ = AR.f32([128, 2, 16]); b_FS = nbuf(r_)
                TT("dve", ZL[:, 0, :], zlr, sm(S_CL1), ALU.mult, [b_W[0], b_SM, b_zcar], [b_ZL])
                TT("dve", ZL[:, 1, :], zli, sm(S_SL1), ALU.mult, [b_W[1], b_SM], [b_ZL])
                TT("dve", ZL[:, 2, :], zlr, sm(S_SL1), ALU.mult, [b_W[0], b_SM], [b_ZL])
                TT("dve", ZL[:, 3, :], zli, sm(S_CL1), ALU.mult, [b_W[1], b_SM], [b_ZL])
                TT("dve", FS[:, 0, :], ZL[:, 0, :], ZL[:, 1, :], ALU.subtract, [b_ZL], [b_FS])
                TT("dve", FS[:, 1, :], ZL[:, 2, :], ZL[:, 3, :], ALU.add, [b_ZL], [b_FS])
                store_rows_T(lambda k: FS[:, 0, :], lambda k: [b_FS], 16, 128, D["sarp"][:, :])
                store_rows_T(lambda k: FS[:, 1, :], lambda k: [b_FS], 16, 128, D["saip"][:, :])

        def stage_D(c):
            WZ, b_W = WZs[c % 2], b_Ws[c % 2]
            c0 = c * LCH
            ti = c // 4
            yb, ybb = banks[4 + c % 2][:, :], bank_buf[4 + c % 2]
            for grp in range(4):
                (pp, ppb) = PP[grp]
                cs = COS[:, grp * 4:(grp + 1) * 4, :].rearrange("p a b -> p (a b)")
                sn = SIN[:, grp * 4:(grp + 1) * 4, :].rearrange("p a b -> p (a b)")
                zre = WZ[:, 0, grp * 4:(grp + 1) * 4, :].rearrange("p a b -> p (a b)")
                zim = WZ[:, 1, grp * 4:(grp + 1) * 4, :].rearrange("p a b -> p (a b)")
                pk = PPK[grp]
                srcs = ((zre, cs, 0, b_COS), (zim, sn, 1, b_SIN), (zre, sn, 0, b_SIN), (zim, cs, 1, b_COS))
                for kk, (zz, tb, wi, tbb) in enumerate(srcs):
                    eng_ = "pool" if (grp, kk) in DEMOD_POOL else "dve"
                    TT(eng_, pp[:, kk, :], zz, tb, ALU.mult, [b_W[wi], tbb], [pk[kk]])
                ct = grp
                yo = yb[:, ct * 128:(ct + 1) * 128]
                MM(yo, DD[:, ct, :], UAB[:, ct, c0:c0 + LCH], True, False, [b_DD, uab[ct][ti]], [ybb])
                for j in range(4):
                    p = grp * 4 + j
                    sl = slice(j * 128, (j + 1) * 128)
                    MM(yo, CR[:, p, :], pp[:, 0, sl], False, False, [b_CR, pk[0]], [ybb])
                    MM(yo, NCR[:, p, :], pp[:, 1, sl], False, False, [b_NCR, pk[1]], [ybb])
                    MM(yo, NCI[:, p, :], pp[:, 2, sl], False, False, [b_NCI, pk[2]], [ybb])
                    MM(yo, NCI[:, p, :], pp[:, 3, sl], False, j == 3, [b_NCI, pk[3]], [ybb])
            glu_front(yb, ybb, LCH, GAs[c % 2], zbank=6 + c % 2)

        emit_bu(0)
        emit_bu(1)
        stage_M(0)
        for c in range(nchunk):
            if c + 1 < nchunk:
                stage_M(c + 1)
            stage_D(c)
            if c >= 1:
                glu_back((c - 1) * LCH, LCH, (c - 1) // 4, GAs[(c - 1) % 2])
        glu_back((nchunk - 1) * LCH, LCH, (nchunk - 1) // 4, GAs[(nchunk - 1) % 2])
        b_W = b_Ws[0] + b_Ws[1]
        if st == 1:
            xoff[0] = X32_WORD0
            S0, b_S0 = xalloc(512, [128, 2, 16, NS], F32)
            SN, b_SN = xalloc(512, [128, 2, 16, NS], F32)
            SNB, b_SNB = xalloc(256, [128, 2, 16, NS], BF16)
            TQ, b_TQ = xalloc(512, [128, 2, 16, NS], F32)
            for ri, nm in enumerate(("sare", "saim")):
                for hf in range(2):
                    load_rows_T(D[nm][:, hf * 1024:(hf + 1) * 1024], NS, 1024,
                                lambda k0, nk, ri=ri, hf=hf: [("act", S0[:, ri, hf * 8 + k0:hf * 8 + k0 + nk, :], [b_S0])],
                                None)
            c0 = 1024
            bre, breb = bank()
            bim, bimb = bank()
            for p in range(16):
                MM(bre[:, p * NS:(p + 1) * NS], BTR[:, p, :], UAB[:, p // 4, c0:c0 + NS], True, True,
                   [b_BTR, uab[p // 4][2]], [breb])
                MM(bim[:, p * NS:(p + 1) * NS], BTI[:, p, :], UAB[:, p // 4, c0:c0 + NS], True, True,
                   [b_BTI, uab[p // 4][2]], [bimb])
            lbr = sm(S_LBR).unsqueeze(2).to_broadcast([128, 16, NS])
            lbi = sm(S_LBI).unsqueeze(2).to_broadcast([128, 16, NS])
            bre_v = bre[:, 0:16 * NS].rearrange("p (a b) -> p a b", a=16)
            bim_v = bim[:, 0:16 * NS].rearrange("p (a b) -> p a b", a=16)
            TT("dve", TQ[:, 0], S0[:, 0], lbr, ALU.mult, [b_S0, b_SM], [b_TQ])
            TT("dve", TQ[:, 1], S0[:, 1], lbi, ALU.mult, [b_S0, b_SM], [b_TQ])
            TT("dve", TQ[:, 0], TQ[:, 0], TQ[:, 1], ALU.subtract, [b_TQ], [b_TQ])
            TT("dve", SN[:, 0], TQ[:, 0], bre_v, ALU.add, [b_TQ, breb], [b_SN])
            TT("dve", TQ[:, 0], S0[:, 1], lbr, ALU.mult, [b_S0, b_SM], [b_TQ])
            TT("dve", TQ[:, 1], S0[:, 0], lbi, ALU.mult, [b_S0, b_SM], [b_TQ])
            TT("dve", TQ[:, 0], TQ[:, 0], TQ[:, 1], ALU.add, [b_TQ], [b_TQ])
            TT("dve", SN[:, 1], TQ[:, 0], bim_v, ALU.add, [b_TQ, bimb], [b_SN])
            CP("act", SNB, SN, [b_SN], [b_SNB])
            for ri, nm in enumerate(("sars", "sais")):
                for hf in range(2):
                    store_rows_T(lambda k, ri=ri, hf=hf: SN[:, ri, hf * 8 + k, :], lambda k: [b_SN], NS, 1024,
                                 D[nm][:, hf * 1024:(hf + 1) * 1024])
            yb, ybb = bank()
            for ct in range(4):
                yo = yb[:, ct * NS:(ct + 1) * NS]
                MM(yo, DD[:, ct, :], UAB[:, ct, c0:c0 + NS], True, False, [b_DD, uab[ct][2]], [ybb])
                for j in range(4):
                    p = ct * 4 + j
                    MM(yo, CR[:, p, :], SNB[:, 0, p, :], False, False, [b_CR, b_SNB], [ybb])
                    MM(yo, NCI[:, p, :], SNB[:, 1, p, :], False, j == 3, [b_NCI, b_SNB], [ybb])
            glu_front(yb, ybb, NS, GAs[0])
            glu_back(c0, NS, 2, GAs[0])
        for k in range(8):
            for t in range(3):
                inherit(xbb[k][t], b_W)
        for k in range(8):
            for t in range(3):
                inherit(x32b[k][t], scratch_bufs)
        load_inputs(st, "32")
        out_proj(st, D["w_out_ab"], lambda k, c0, n: YM[:, k, c0:c0 + n], lambda k, ti: ymb[k][ti], 0)
        AR.release(m0)

    for st in range(2):
        if "noin" not in stages and (st == 0 or "f1" not in stages):
            load_inputs(st, "b")
            if "me" not in stages:
                load_inputs(st, "32")
        if "me" in stages:
            mixer_even(st, pre_s5_hook=(lambda: store_outputs(0)) if (st == 1 and "noout" not in stages) else None)
        if "f0" in stages:
            ffn(st, 0)
        if "mo" in stages:
            mixer_odd(st)
        if "f1" in stages:
            ffn(st, 1, mid_hook=(lambda: load_inputs(1, "b")) if st == 0 else None)
        if "noout" not in stages and (st == 1 or "me" not in stages):
            store_outputs(st)
    kb.emit()
    return nc, kb


_CACHE = {}


def kernel(**inp):
    inp = {k: np.asarray(v) for k, v in inp.items()}
    if "nc" not in _CACHE:
        _CACHE["nc"], _CACHE["kb"] = build_program()
    nc = _CACHE["nc"]
    shared = host_params(inp)
    in_maps = []
    for i in range(NCORES):
        s0, s1 = NS * i, NS * (i + 1)
        m = dict(shared)
        m["xp"] = np.ascontiguousarray(inp["x_prompt"][i])
        m["xs"] = np.ascontiguousarray(inp["x_sample"][s0:s1, 0, :])
        m["sare"] = np.ascontiguousarray(inp["state_a_re"][0, s0:s1].reshape(NS, 2048))
        m["saim"] = np.ascontiguousarray(inp["state_a_im"][0, s0:s1].reshape(NS, 2048))
        m["ccc"] = np.ascontiguousarray(inp["cache_c_conv"][0, s0:s1].reshape(NS * 30, DM))
        m["cfc"] = np.ascontiguousarray(inp["cache_ffn_conv"][:, s0:s1].reshape(2, NS * 2, DFF))
        in_maps.append({k: np.ascontiguousarray(v, dtype=np.float32) for k, v in m.items()})
    res = run_bass_kernel_spmd(nc, in_maps, core_ids=list(range(NCORES)))
    R = res.results
    f32 = np.float32
    yp = np.stack([R[i]["yp"] for i in range(NCORES)]).astype(f32)
    ys = np.concatenate([R[i]["ys"] for i in range(NCORES)])[:, None, :].astype(f32)
    sarp = np.stack([R[i]["sarp"].reshape(32, 64) for i in range(NCORES)])[None].astype(f32)
    saip = np.stack([R[i]["saip"].reshape(32, 64) for i in range(NCORES)])[None].astype(f32)
    sars = np.concatenate([R[i]["sars"].reshape(NS, 32, 64) for i in range(NCORES)])[None].astype(f32)
    sais = np.concatenate([R[i]["sais"].reshape(NS, 32, 64) for i in range(NCORES)])[None].astype(f32)
    sbv = np.concatenate([R[i]["sbv"] for i in range(NCORES)])[None, :, None, :].astype(f32)
    ccp = np.stack([R[i]["ccp"] for i in range(NCORES)])[None].astype(f32)
    ccs = np.concatenate([R[i]["ccs"].reshape(NS, 30, DM) for i in range(NCORES)])[None].astype(f32)
    cfp = np.stack([R[i]["cfp"] for i in range(NCORES)], axis=1).astype(f32)
    cfs = np.concatenate([R[i]["cfs"].reshape(2, NS, 2, DFF) for i in range(NCORES)], axis=1).astype(f32)
    return (yp, ys, sarp, saip, sars, sais, sbv, ccp, ccs, cfp, cfs)
```
